# Optimizing a Trainium2 kernel written in Bass

```python
import jax, jax.numpy as jnp
from jax import lax
import numpy as np

D_MODEL = 1024
BATCH = 4
SEQ = 8192
DEPTH = 4

N_MIXERS = 2
PLE_DIM = 256
EPS = 1e-6
NEG_BIG = -1e30

A_HEADS = 16
A_KV_HEADS = 4
A_GROUP = A_HEADS // A_KV_HEADS
A_HEAD_DIM = D_MODEL // A_HEADS
A_WIDTH = A_HEADS * A_HEAD_DIM
A_KV_WIDTH = A_KV_HEADS * A_HEAD_DIM
A_IN = A_WIDTH + 2 * A_KV_WIDTH + A_WIDTH
WINDOW = 128
BLOCK = 128

B_HEADS = 16
B_NOPE = 64
B_ROPE = 32
B_VDIM = 64
B_WIDTH = B_HEADS * B_VDIM
Q_LORA = 384
KV_LORA = 256
B_IN = Q_LORA + KV_LORA + B_ROPE + B_WIDTH
ROPE_THETA = 10000.0
Q_BLOCK = 128

N_A = (DEPTH + 1) // 2
N_B = DEPTH // 2

kernel_name = "hybrid_swa_sink_alibi_mla_encoder"


def rms_norm(x, g):
    xf = x.astype(jnp.float32)
    y = xf * lax.rsqrt(jnp.mean(xf * xf, axis=-1, keepdims=True) + EPS)
    return (y * g.astype(jnp.float32)).astype(x.dtype)


def alibi_slopes(n):
    return 2.0 ** (-8.0 * jnp.arange(1, n + 1, dtype=jnp.float32) / n)


def rotate(x, cos, sin):
    half = x.shape[-1] // 2
    x1, x2 = x[..., :half], x[..., half:]
    return jnp.concatenate([x1 * cos - x2 * sin, x1 * sin + x2 * cos], axis=-1)


def windowed_gqa(u, w_in, sink, w_out):
    B_, S, _ = u.shape
    nb = S // BLOCK
    proj = u @ w_in
    q, k, v, z = jnp.split(proj, [A_WIDTH, A_WIDTH + A_KV_WIDTH, A_WIDTH + 2 * A_KV_WIDTH], axis=-1)
    q = q.reshape(B_, nb, BLOCK, A_KV_HEADS, A_GROUP, A_HEAD_DIM).swapaxes(0, 1)
    k = k.reshape(B_, S, A_KV_HEADS, A_HEAD_DIM)
    v = v.reshape(B_, S, A_KV_HEADS, A_HEAD_DIM)
    pad = ((0, 0), (BLOCK, BLOCK), (0, 0), (0, 0))
    kp = jnp.pad(k, pad)
    vp = jnp.pad(v, pad)
    scale = A_HEAD_DIM ** -0.5
    slopes = alibi_slopes(A_HEADS).reshape(A_KV_HEADS, A_GROUP)
    sink_l = sink.astype(jnp.float32).reshape(A_KV_HEADS, A_GROUP)
    qi = jnp.arange(BLOCK)
    kj = jnp.arange(3 * BLOCK)
    rel = (kj[None, :] - BLOCK) - qi[:, None]
    dist = jnp.abs(rel).astype(jnp.float32)
    in_window = jnp.abs(rel) <= WINDOW

    def block_fn(args):
        q_blk, b = args
        start = b * BLOCK
        k_blk = lax.dynamic_slice_in_dim(kp, start, 3 * BLOCK, axis=1)
        v_blk = lax.dynamic_slice_in_dim(vp, start, 3 * BLOCK, axis=1)
        s_pos = start - BLOCK + kj
        valid = in_window & ((s_pos >= 0) & (s_pos < S))[None, :]
        logits = jnp.einsum('bqkgd,bskd->bkgqs', q_blk, k_blk).astype(jnp.float32) * scale
        logits = logits - slopes[:, :, None, None] * dist
        logits = jnp.where(valid, logits, NEG_BIG)
        sink_col = jnp.broadcast_to(sink_l[None, :, :, None, None], logits.shape[:-1] + (1,))
        probs = jax.nn.softmax(jnp.concatenate([logits, sink_col], axis=-1), axis=-1)[..., :-1]
        return jnp.einsum('bkgqs,bskd->bqkgd', probs.astype(v_blk.dtype), v_blk)

    o = lax.map(block_fn, (q, jnp.arange(nb)))
    o = o.swapaxes(0, 1).reshape(B_, S, A_WIDTH)
    return (o * jax.nn.silu(z)) @ w_out


def mla(u, w_in, q_norm, w_qb, kv_norm, w_kvb, w_out):
    B_, S, _ = u.shape
    nb = S // Q_BLOCK
    proj = u @ w_in
    cq, ckv, k_rope, z = jnp.split(proj, [Q_LORA, Q_LORA + KV_LORA, Q_LORA + KV_LORA + B_ROPE], axis=-1)
    q = (rms_norm(cq, q_norm) @ w_qb).reshape(B_, S, B_HEADS, B_NOPE + B_ROPE)
    q_nope, q_rope = q[..., :B_NOPE], q[..., B_NOPE:]
    kv = (rms_norm(ckv, kv_norm) @ w_kvb).reshape(B_, S, B_HEADS, B_NOPE + B_VDIM)
    k_nope, v = kv[..., :B_NOPE], kv[..., B_NOPE:]
    half = B_ROPE // 2
    inv_freq = ROPE_THETA ** (-jnp.arange(half, dtype=jnp.float32) / half)
    ang = jnp.arange(S, dtype=jnp.float32)[:, None] * inv_freq[None, :]
    cos, sin = jnp.cos(ang).astype(u.dtype), jnp.sin(ang).astype(u.dtype)
    q_rope = rotate(q_rope, cos[:, None, :], sin[:, None, :])
    k_rope = rotate(k_rope, cos, sin)
    scale = (B_NOPE + B_ROPE) ** -0.5
    qn = q_nope.reshape(B_, nb, Q_BLOCK, B_HEADS, B_NOPE).swapaxes(0, 1)
    qr = q_rope.reshape(B_, nb, Q_BLOCK, B_HEADS, B_ROPE).swapaxes(0, 1)

    def block_fn(args):
        qn_b, qr_b = args
        logits = (jnp.einsum('bqhd,bshd->bhqs', qn_b, k_nope)
                  + jnp.einsum('bqhr,bsr->bhqs', qr_b, k_rope)).astype(jnp.float32) * scale
        probs = jax.nn.softmax(logits, axis=-1)
        return jnp.einsum('bhqs,bshd->bqhd', probs.astype(v.dtype), v)

    o = lax.map(block_fn, (qn, qr)).swapaxes(0, 1).reshape(B_, S, B_WIDTH)
    return (o * jax.nn.silu(z)) @ w_out


def setup_inputs(seed: int = 0) -> dict:
    key = jax.random.key(seed)
    ks = jax.random.split(key, 18)
    f32 = jnp.float32
    nrm = lambda k, shape, s: jax.random.normal(k, shape, f32) * s
    gain = lambda k, shape: 1.0 + 0.05 * jax.random.normal(k, shape, f32)
    return {
        "x": jax.random.normal(ks[0], (BATCH, SEQ, D_MODEL), f32),
        "p": jax.random.normal(ks[1], (DEPTH, BATCH, SEQ, PLE_DIM), f32),
        "norm_g": gain(ks[2], (DEPTH, D_MODEL)),
        "a_w_in": nrm(ks[3], (N_A, D_MODEL, A_IN), D_MODEL ** -0.5),
        "a_sink": nrm(ks[4], (N_A, A_HEADS), 1.0),
        "a_w_out": nrm(ks[5], (N_A, A_WIDTH, D_MODEL), A_WIDTH ** -0.5),
        "b_w_in": nrm(ks[6], (N_B, D_MODEL, B_IN), D_MODEL ** -0.5),
        "b_q_norm": gain(ks[7], (N_B, Q_LORA)),
        "b_w_qb": nrm(ks[8], (N_B, Q_LORA, B_HEADS * (B_NOPE + B_ROPE)), Q_LORA ** -0.5),
        "b_kv_norm": gain(ks[9], (N_B, KV_LORA)),
        "b_w_kvb": nrm(ks[10], (N_B, KV_LORA, B_HEADS * (B_NOPE + B_VDIM)), KV_LORA ** -0.5),
        "b_w_out": nrm(ks[11], (N_B, B_WIDTH, D_MODEL), B_WIDTH ** -0.5),
        "ple_w": nrm(ks[12], (DEPTH, PLE_DIM, D_MODEL), PLE_DIM ** -0.5),
        "ple_norm_g": gain(ks[13], (DEPTH, D_MODEL)),
        "ple_w_gate": nrm(ks[14], (DEPTH, D_MODEL, D_MODEL), D_MODEL ** -0.5),
        "final_norm_g": gain(ks[15], (D_MODEL,)),
    }


def reference(x, p, norm_g, a_w_in, a_sink, a_w_out, b_w_in, b_q_norm, b_w_qb,
              b_kv_norm, b_w_kvb, b_w_out, ple_w, ple_norm_g, ple_w_gate, final_norm_g):
    h = x
    for i in range(DEPTH):
        u = rms_norm(h, norm_g[i])
        j = i // N_MIXERS
        if i % N_MIXERS == 0:
            y = windowed_gqa(u, a_w_in[j], a_sink[j], a_w_out[j])
        else:
            y = mla(u, b_w_in[j], b_q_norm[j], b_w_qb[j], b_kv_norm[j], b_w_kvb[j], b_w_out[j])
        h = h + y
        gate = jax.nn.sigmoid(rms_norm(h, ple_norm_g[i]) @ ple_w_gate[i])
        h = h + (p[i] @ ple_w[i]) * gate
    return rms_norm(h, final_norm_g)
```

```python
import numpy as np
import concourse.bass as bass
import concourse.mybir as mybir
from concourse.bass_utils import run_bass_kernel_spmd

F32 = mybir.dt.float32
BF16 = mybir.dt.bfloat16
AF = mybir.ActivationFunctionType
ALU = mybir.AluOpType

NCORES = 8
D = 1024
TL = 4096
SEQ = 8192
TG = 512
NG = TL // TG
EPS = 1e-6
SB_LO = 16640
SB_HI = 229376

V_NORM = 0
V_PLE = 32
V_FIN = 64
V_QN = 72
V_KVN = 78
V_N = 82


class Tracker:
    ENGS = ["sp", "act", "dve", "pool", "pe"]

    def __init__(self, nc, n_dma_sems=24):
        self.nc = nc
        self.I = []
        self.lw = {}
        self.rd = {}
        self.nds = n_dma_sems
        self.dma_count = 0
        self.dma_last = {}
        self.cc_count = 0
        self.cc_last = None
        self.last_stream = {}
        self.bar = {e: {} for e in self.ENGS}

    def _stream(self, ins):
        if ins["kind"] == "c":
            return ins["eng"]
        if ins["kind"] == "dma":
            return ("dma", ins["slot"])
        return ("cc",)

    MAXI = None

    def add(self, eng, fn, reads=(), writes=(), kind="c"):
        i = len(self.I)
        if Tracker.MAXI is not None and i >= Tracker.MAXI and fn is not None:
            return None
        ins = dict(eng=eng, fn=fn, kind=kind, sig=False)
        deps = {}

        def dep(j):
            s = self._stream(self.I[j])
            if deps.get(s, -1) < j:
                deps[s] = j

        if kind == "dma":
            slot = self.dma_count % self.nds
            ins["slot"] = slot
            ins["target"] = 16 * (self.dma_count // self.nds + 1)
            self.dma_count += 1
            if slot in self.dma_last:
                dep(self.dma_last[slot])
            self.dma_last[slot] = i
        elif kind == "cc":
            self.cc_count += 1
            ins["target"] = self.cc_count
            if self.cc_last is not None:
                dep(self.cc_last)
            self.cc_last = i
        me = self._stream(ins)

        def same_c(j):
            o = self.I[j]
            return kind == "c" and o["kind"] == "c" and o["eng"] == eng

        for t in reads:
            for s_, w in self.lw.get(t, {}).items():
                if same_c(w) and eng == "pe":
                    continue
                dep(w)
        for t in writes:
            for s_, w in self.lw.get(t, {}).items():
                if not (same_c(w) and eng == "pe"):
                    dep(w)
            for s_, r in self.rd.get(t, {}).items():
                dep(r)
        for s_, j in self.bar[eng].items():
            if deps.get(s_, -1) < j:
                deps[s_] = j
        self.bar[eng] = {}
        ins["deps"] = deps
        self.I.append(ins)
        for t in reads:
            self.rd.setdefault(t, {})[me] = i
        for t in writes:
            self.lw.setdefault(t, {})[me] = i
        self.last_stream[me] = i
        return i

    def barrier(self):
        for e in self.ENGS:
            for s, j in self.last_stream.items():
                if self.bar[e].get(s, -1) < j:
                    self.bar[e][s] = j

    def finish(self):
        self.barrier()
        self.add("sp", None)

    def emit(self):
        nc = self.nc
        for ins in self.I:
            for s, j in ins["deps"].items():
                self.I[j]["sig"] = True
        cnt = {e: 0 for e in self.ENGS}
        for ins in self.I:
            if ins["kind"] == "c" and ins["sig"]:
                cnt[ins["eng"]] += 1
                ins["tick"] = cnt[ins["eng"]]
        csem = {e: nc.alloc_semaphore(f"c_{e}") for e in ["act", "dve", "pool", "pe"]}
        dsem = [nc.alloc_semaphore(f"d_{k}") for k in range(self.nds)]
        ccsem = nc.alloc_semaphore("ccsem")
        I = self.I

        def run(engname, e):
            waited = {}
            for ins in I:
                if ins["eng"] != engname:
                    continue
                for s, j in ins["deps"].items():
                    p = I[j]
                    if p["kind"] == "c":
                        sem, val = csem[p["eng"]], p["tick"]
                    elif p["kind"] == "dma":
                        sem, val = dsem[p["slot"]], p["target"]
                    else:
                        sem, val = ccsem, p["target"]
                    if waited.get(s, 0) >= val:
                        continue
                    waited[s] = val
                    e.wait_ge(sem, val)
                if ins["fn"] is None:
                    continue
                bi = ins["fn"](e)
                if ins["kind"] == "dma":
                    bi.then_inc(dsem[ins["slot"]], 16)
                elif ins["kind"] == "cc":
                    bi.then_inc(ccsem, 1)
                elif ins["sig"]:
                    bi.then_inc(csem[engname], 1)

        with nc.Block() as block:
            @block.sync
            def _(e):
                run("sp", e)

            @block.scalar
            def _(e):
                run("act", e)

            @block.vector
            def _(e):
                run("dve", e)

            @block.gpsimd
            def _(e):
                run("pool", e)

            @block.tensor
            def _(e):
                run("pe", e)


class Arena:
    def __init__(self, nc):
        self.nc = nc
        self.cur = SB_LO
        self.uid = 0

    def alloc(self, name, shape, dtype):
        n = 1
        for s in shape[1:]:
            n *= s
        nb = n * (4 if dtype == F32 else 2)
        nb = (nb + 31) // 32 * 32
        off = self.cur
        assert off + nb <= SB_HI, f"SBUF overflow at {name}: {off + nb}"
        self.cur = off + nb
        self.uid += 1
        h = self.nc.alloc_sbuf_tensor_at(f"{name}_{self.uid}", list(shape), dtype, offset=off)
        return h.ap()

    def mark(self):
        return self.cur

    def reset(self, m):
        self.cur = m


def alibi_slope(h):
    return float(2.0 ** (-8.0 * (h + 1) / 16.0))


def build_program(layers, final, ncores=NCORES):
    nc = bass.Bass("TRN2", target_bir_lowering=False)
    T = Tracker(nc)
    A = Arena(nc)

    def din(name, shape, dt=F32):
        return nc.dram_tensor(name, list(shape), dt, kind="ExternalInput").ap()

    hin = din("hin", [D, TL])
    pT = din("pT", [4, 256, TL])
    vecs_d = din("vecs", [128, V_N])
    sink_d = din("sinkrep", [128, 32])
    distm_d = din("distm", [128, 384])
    sel_d = din("sel", [128, 4])
    ropeC_d = din("ropeC", [128, TL])
    ropeS_d = din("ropeS", [128, TL])
    a_w_in = din("a_w_in2", [2, D, 2816])
    a_w_out = din("a_w_out", [2, D, D])
    b_w_in = din("b_w_in2", [2, D, 1728])
    b_w_qb = din("b_w_qb2", [2, 384, 2048])
    b_w_kvb = din("b_w_kvb2", [2, 256, 2048])
    b_w_out = din("b_w_out", [2, D, D])
    ple_w = din("ple_w", [4, 256, D])
    ple_wg = din("ple_w_gate", [4, D, D])
    hout = nc.dram_tensor("hout", [D, TL], F32, kind="ExternalOutput").ap()

    hmid = nc.dram_tensor("hmid", [D, TL], F32).ap()
    QA = nc.dram_tensor("QA", [D, TL], BF16).ap()
    ZS = nc.dram_tensor("ZS", [D, TL], BF16).ap()
    AGW = 2176
    agA_src = nc.dram_tensor("agA_src", [128, AGW], BF16).ap()
    agA_dst = nc.dram_tensor("agA_dst", [256, AGW], BF16).ap()
    QN = nc.dram_tensor("QN", [D, TL], BF16).ap()
    QR = nc.dram_tensor("QR", [512, TL], BF16).ap()
    lat_src = [nc.dram_tensor(f"lat_src{c}", [128 if c < 2 else 32, TL], BF16).ap() for c in range(3)]
    lat_all = [nc.dram_tensor(f"lat_all{c}", [256 if c < 2 else 64, TL], BF16).ap() for c in range(3)]
    KN = nc.dram_tensor("KN", [D, SEQ], BF16).ap()
    VS = nc.dram_tensor("VS", [16, 128, 64, 64], BF16).ap()

    ps = nc.alloc_psum_tensor("ps", [128, 8, 512], F32).ap()
    RG = [[2 * i, 2 * i + 1] for i in range(ncores // 2)]

    state = dict(bank=0, k=0)

    def nb():
        b = state["bank"]
        state["bank"] = (b + 1) % 8
        return b

    def uniq():
        state["k"] += 1
        return state["k"]

    def dma(out, in_, reads=(), writes=()):
        T.add("sp", lambda e, out=out, in_=in_: e.dma_start(out=out, in_=in_),
              reads=reads, writes=writes, kind="dma")

    def mm(out, lhsT, rhs, start, stop, reads=(), writes=()):
        T.add("pe", lambda e, out=out, lhsT=lhsT, rhs=rhs, start=start, stop=stop:
              e.matmul(out, lhsT, rhs, start=start, stop=stop), reads=reads, writes=writes)

    def act(out, in_, func, reads=(), writes=(), scale=1.0):
        T.add("act", lambda e, out=out, in_=in_, func=func, scale=scale:
              e.activation(out=out, in_=in_, func=func, scale=scale), reads=reads, writes=writes)

    def tt(eng, out, in0, in1, op, reads=(), writes=()):
        T.add(eng, lambda e, out=out, in0=in0, in1=in1, op=op:
              e.tensor_tensor(out=out, in0=in0, in1=in1, op=op), reads=reads, writes=writes)

    def ts(eng, out, in0, s1, s2, op0, op1=None, reads=(), writes=()):
        if op1 is None:
            T.add(eng, lambda e, out=out, in0=in0, s1=s1, op0=op0:
                  e.tensor_scalar(out=out, in0=in0, scalar1=s1, scalar2=None, op0=op0),
                  reads=reads, writes=writes)
        else:
            T.add(eng, lambda e, out=out, in0=in0, s1=s1, s2=s2, op0=op0, op1=op1:
                  e.tensor_scalar(out=out, in0=in0, scalar1=s1, scalar2=s2, op0=op0, op1=op1),
                  reads=reads, writes=writes)

    def stt(eng, out, in0, scalar, in1, op0, op1, reads=(), writes=()):
        T.add(eng, lambda e, out=out, in0=in0, scalar=scalar, in1=in1, op0=op0, op1=op1:
              e.scalar_tensor_tensor(out=out, in0=in0, scalar=scalar, in1=in1, op0=op0, op1=op1),
              reads=reads, writes=writes)

    def cp(eng, out, in_, reads=(), writes=()):
        T.add(eng, lambda e, out=out, in_=in_: e.tensor_copy(out=out, in_=in_),
              reads=reads, writes=writes)

    def mset(eng, ap, val, writes=()):
        T.add(eng, lambda e, ap=ap, val=val: e.memset(ap, val), writes=writes)

    ones = A.alloc("ones", [128, 128], BF16)
    vecs = A.alloc("vecs", [128, V_N], F32)
    gf32 = A.alloc("gf32", [128, 8], F32)
    sinkrep = A.alloc("sinkrep", [128, 32], F32)
    esink = A.alloc("esink", [128, 32], F32)
    distm = A.alloc("distm", [128, 384], F32)
    sel = A.alloc("sel", [128, 4], F32)
    mset("pool", ones, 1.0, writes=["ones"])
    dma(vecs, vecs_d, writes=["vecs"])
    dma(sinkrep, sink_d, writes=["sinkrep"])
    dma(distm, distm_d, writes=["distm"])
    dma(sel, sel_d, writes=["sel"])
    ts("pool", gf32, vecs[:, V_FIN:V_FIN + 8], 1.0, None, ALU.mult, reads=["vecs"], writes=["gf32"])
    act(esink, sinkrep, AF.Exp, reads=["sinkrep"], writes=["esink"])
    base_mark = A.mark()

    def load_w(dst, src, KC, N, gcol, tok, wst):
        PW = 1024
        pieces = [(n0, min(n0 + PW, N)) for n0 in range(0, N, PW)]
        for c in range(KC):
            for (n0, n1) in pieces:
                k = uniq() % len(wst)
                w = n1 - n0
                dma(wst[k][:, 0:w], src[c * 128:(c + 1) * 128, n0:n1], writes=[("wst", k)])
                s1 = vecs[:, gcol + c:gcol + c + 1] if gcol is not None else 1.0
                if k % 2 == 0:
                    ts("dve", dst[:, c, n0:n1], wst[k][:, 0:w], s1, None, ALU.mult,
                       reads=[("wst", k), "vecs"], writes=[tok])
                else:
                    act(dst[:, c, n0:n1], wst[k][:, 0:w], AF.Copy, scale=s1,
                        reads=[("wst", k), "vecs"], writes=[tok])

    def norm_gen(src, stok, dst, dtok, nch, r, sqb, wts, dim, mult, rtok="r", nbk=None, sqtok="sq"):
        bank = (nbk or nb)()
        pend = []
        for c in range(nch):
            k = c % 2
            eng = "pool" if c % 2 == 0 else "dve"
            tt(eng, sqb[k], src[:, c, :], src[:, c, :], ALU.mult,
               reads=[(stok, c)], writes=[(sqtok, k)])
            pend.append((c, k))
            if len(pend) == 2 or c == nch - 1:
                yield
                for (c_, k_) in pend:
                    mm(ps[:, bank, 0:wts], ones, sqb[k_][:, 0:wts], c_ == 0, c_ == nch - 1,
                       reads=[(sqtok, k_), "ones"], writes=[("ps", bank)])
                pend = []
        yield
        m2 = float(mult) ** 2
        ts("dve", r, ps[:, bank, 0:wts], 1.0 / m2, float(dim * EPS) / m2, ALU.mult, ALU.add,
           reads=[("ps", bank)], writes=[rtok])
        yield
        act(r, r, AF.Ln, reads=[rtok], writes=[rtok])
        act(r, r, AF.Exp, scale=-0.5, reads=[rtok], writes=[rtok])
        yield
        for c in range(nch):
            eng = "dve" if c % 2 == 0 else "pool"
            tt(eng, dst[:, c, :], src[:, c, :], r, ALU.mult,
               reads=[(stok, c), rtok], writes=[(dtok, c)])
            if c % 4 == 3:
                yield

    def norm_group(*a, **kw):
        for _ in norm_gen(*a, **kw):
            pass

    def hview(ap, g):
        return ap.rearrange("(c p) t -> p c t", p=128)[:, :, g * TG:(g + 1) * TG]

    def tail_gen(li, g, G, gtok, hsrc, hdst, tw, is_last, nbk=None):
        nbk = nbk or nb
        hT, sqb, r, n2, pst, pbf, gate, tmpb, obuf = (tw[k] for k in
            ["hT", "sqb", "r", "n2", "pst", "pbf", "gate", "tmp", "obuf"])
        Wo, Wg, Wp = tw["Wo"], tw["Wg"], tw["Wp"]
        dma(hT, hview(hsrc, g), reads=[("hd", g)], writes=[("h", c) for c in range(8)])
        dma(pst, pT[li].rearrange("(c p) t -> p c t", p=128)[:, :, g * TG:(g + 1) * TG],
            writes=["pst"])
        yield
        cp("pool", pbf, pst, reads=["pst"], writes=["pbf"])
        banks = {}
        for oc in range(9):
            if oc < 8:
                bank = nbk()
                banks[oc] = bank
                for c in range(8):
                    mm(ps[:, bank, :], Wo[:, c, oc * 128:(oc + 1) * 128], G[:, c, :], c == 0, c == 7,
                       reads=["Wo", (gtok, c)], writes=[("ps", bank)])
            if oc >= 1:
                o = oc - 1
                tt("dve", hT[:, o, :], ps[:, banks[o], :], hT[:, o, :], ALU.add,
                   reads=[("ps", banks[o]), ("h", o)], writes=[("h", o)])
            yield
        for _ in norm_gen(hT, "h", n2, "n2", 8, r, sqb, TG, D, 32.0, nbk=nbk):
            yield
        b1s, b2s = {}, {}
        for oc in range(9):
            if oc < 8:
                b1 = nbk()
                b1s[oc] = b1
                for c in range(8):
                    mm(ps[:, b1, :], Wg[:, c, oc * 128:(oc + 1) * 128], n2[:, c, :], c == 0, c == 7,
                       reads=["Wg", ("n2", c)], writes=[("ps", b1)])
            if 1 <= oc < 9:
                o = oc - 1
                act(gate[0], ps[:, b1s[o], :], AF.Sigmoid, reads=[("ps", b1s[o])], writes=[("gate", 0)])
                b2 = nbk()
                b2s[o] = b2
                for c in range(2):
                    mm(ps[:, b2, :], Wp[:, c, o * 128:(o + 1) * 128], pbf[:, c, :], c == 0, c == 1,
                       reads=["Wp", "pbf"], writes=[("ps", b2)])
                tt("dve", tmpb[0], ps[:, b2, :], gate[0], ALU.mult,
                   reads=[("ps", b2), ("gate", 0)], writes=[("tmp", 0)])
                tt("pool", hT[:, o, :], hT[:, o, :], tmpb[0], ALU.add,
                   reads=[("h", o), ("tmp", 0)], writes=[("h", o)])
            yield
        if not is_last:
            dma(hview(hdst, g), hT, reads=[("h", c) for c in range(8)], writes=[("hd", g)])
        else:
            bank = nbk()
            for c in range(8):
                k = c % 2
                tt("pool" if c % 2 == 0 else "dve", sqb[k], hT[:, c, :], hT[:, c, :], ALU.mult,
                   reads=[("h", c)], writes=[("sq", k)])
                mm(ps[:, bank, :], ones, sqb[k], c == 0, c == 7,
                   reads=[("sq", k), "ones"], writes=[("ps", bank)])
            ts("dve", r, ps[:, bank, :], 1.0 / 1024.0, float(EPS), ALU.mult, ALU.add,
               reads=[("ps", bank)], writes=["r"])
            act(r, r, AF.Ln, reads=["r"], writes=["r"])
            act(r, r, AF.Exp, scale=-0.5, reads=["r"], writes=["r"])
            for c in range(8):
                k = c % 2
                stt("dve", obuf[k], hT[:, c, :], gf32[:, c:c + 1], r, ALU.mult, ALU.mult,
                    reads=[("h", c), "r", "gf32"], writes=[("ob", k)])
                dma(hout[c * 128:(c + 1) * 128, g * TG:(g + 1) * TG], obuf[k],
                    reads=[("ob", k)], writes=[("hd", g)])

    def tail(*a, **kw):
        for _ in tail_gen(*a, **kw):
            pass

    def alloc_tail(li, wo_src, is_last):
        tw = {}
        tw["Wo"] = A.alloc("Wo", [128, 8, D], BF16)
        tw["Wg"] = A.alloc("Wg", [128, 8, D], BF16)
        tw["Wp"] = A.alloc("Wp", [128, 2, D], BF16)
        tw["hT"] = A.alloc("hT", [128, 8, TG], F32)
        tw["sqb"] = [A.alloc("sq", [128, TG], BF16) for _ in range(2)]
        tw["r"] = A.alloc("r", [128, TG], F32)
        tw["n2"] = A.alloc("n2", [128, 8, TG], BF16)
        tw["pst"] = A.alloc("pst", [128, 2, TG], F32)
        tw["pbf"] = A.alloc("pbf", [128, 2, TG], BF16)
        tw["gate"] = [A.alloc("gate", [128, TG], F32)] * 2
        tw["tmp"] = [A.alloc("tmp", [128, TG], F32)] * 2
        tw["obuf"] = [A.alloc("obuf", [128, TG], F32) for _ in range(2)] if is_last else None
        return tw

    def load_tail_w(li, tw, wo_src, wst):
        load_w(tw["Wo"], wo_src, 8, D, None, "Wo", wst)
        load_w(tw["Wg"], ple_wg[li], 8, D, V_PLE + 8 * li, "Wg", wst)
        load_w(tw["Wp"], ple_w[li], 2, D, None, "Wp", wst)

    def layer_A(li, hsrc, hdst, is_last):
        j = li // 2
        A.reset(base_mark)
        Kd = A.alloc("Kd", [128, 4, 34 * 128], BF16)
        Va = A.alloc("Va", [128, 34, 576], BF16)
        pm = A.mark()
        W = A.alloc("Win", [128, 8, 2816], BF16)
        wst = [A.alloc("wst", [128, 1024], F32) for _ in range(6)]
        hTs = [A.alloc("hT", [128, 8, TG], F32) for _ in range(2)]
        sqb = [A.alloc("sq", [128, TG], BF16) for _ in range(2)]
        rs = [A.alloc("r", [128, TG], F32) for _ in range(2)]
        us = [A.alloc("u", [128, 8, TG], BF16) for _ in range(2)]
        qst = [A.alloc("qst", [128, TG], BF16) for _ in range(3)]
        zst = [A.alloc("zst", [128, TG], BF16) for _ in range(3)]
        mset("pool", Va, 1.0, writes=["Va"])
        load_w(W, a_w_in[j], 8, 2816, V_NORM + 8 * li, "W", wst)

        def prep_load(g):
            par = g % 2
            dma(hTs[par], hview(hsrc, g), reads=[("hd", g)],
                writes=[(f"h{par}", c) for c in range(8)])

        def prep_norm(g):
            par = g % 2
            norm_group(hTs[par], f"h{par}", us[par], f"u{par}", 8, rs[par], sqb, TG, D, 32.0,
                       rtok=f"r{par}")
        prep_load(0)
        prep_norm(0)
        for g in range(NG):
            if g + 1 < NG:
                prep_load(g + 1)
            u = us[g % 2]
            ut = f"u{g % 2}"
            for oc in range(8):
                bank = nb()
                for c in range(8):
                    mm(ps[:, bank, :], W[:, c, oc * 128:(oc + 1) * 128], u[:, c, :], c == 0, c == 7,
                       reads=["W", (ut, c)], writes=[("ps", bank)])
                k = uniq() % 3
                T.add("act", lambda e, o=qst[k], i_=ps[:, bank, :]: e.mul(o, i_, 0.125),
                      reads=[("ps", bank)], writes=[("qst", k)])
                dma(QA[oc * 128:(oc + 1) * 128, g * TG:(g + 1) * TG], qst[k],
                    reads=[("qst", k)], writes=[("QA", g)])
            if g + 1 < NG:
                prep_norm(g + 1)
            for kvh in range(4):
                bank = nb()
                co = 1024 + kvh * 128
                for c in range(8):
                    mm(ps[:, bank, :], W[:, c, co:co + 128], u[:, c, :], c == 0, c == 7,
                       reads=["W", (ut, c)], writes=[("ps", bank)])
                cp("dve", Kd[:, kvh, (1 + 4 * g) * 128:(5 + 4 * g) * 128], ps[:, bank, :],
                   reads=[("ps", bank)], writes=["Kd"])
            for blk in range(4):
                bank = nb()
                for c in range(8):
                    mm(ps[:, bank, 0:256], u[:, c, blk * 128:(blk + 1) * 128], W[:, c, 1536:1792],
                       c == 0, c == 7, reads=["W", (ut, c)], writes=[("ps", bank)])
                dst = Va[:, 1 + 4 * g + blk, 64:576].rearrange("p (k x) -> p k x", x=128)[:, :, 0:64]
                cp("dve", dst, ps[:, bank, 0:256].rearrange("p (k x) -> p k x", x=64),
                   reads=[("ps", bank)], writes=["Va"])
            for oc in range(8):
                bank = nb()
                co = 1792 + oc * 128
                for c in range(8):
                    mm(ps[:, bank, :], W[:, c, co:co + 128], u[:, c, :], c == 0, c == 7,
                       reads=["W", (ut, c)], writes=[("ps", bank)])
                k = uniq() % 3
                act(zst[k], ps[:, bank, :], AF.Silu, reads=[("ps", bank)], writes=[("zst", k)])
                dma(ZS[oc * 128:(oc + 1) * 128, g * TG:(g + 1) * TG], zst[k],
                    reads=[("zst", k)], writes=[("ZS", g)])
        dma(agA_src[:, 0:512].rearrange("p (k x) -> p k x", x=128), Kd[:, :, 128:256],
            reads=["Kd"], writes=["agsrc"])
        dma(agA_src[:, 512:1024].rearrange("p (k x) -> p k x", x=128), Kd[:, :, 32 * 128:33 * 128],
            reads=["Kd"], writes=["agsrc"])
        dma(agA_src[:, 1024:1600], Va[:, 1, :], reads=["Va"], writes=["agsrc"])
        dma(agA_src[:, 1600:2176], Va[:, 32, :], reads=["Va"], writes=["agsrc"])
        T.barrier()
        T.add("pool", lambda e: e.collective_compute("AllGather", ALU.bypass, replica_groups=RG,
                                                     ins=[agA_src], outs=[agA_dst]),
              reads=["agsrc"], writes=["agdst"], kind="cc")
        A.reset(pm)
        GA = A.alloc("GA", [128, 2, AGW], BF16)
        htmp = A.alloc("htmp", [128, 576], BF16)
        dma(GA, agA_dst.rearrange("(r p) w -> p r w", p=128), reads=["agdst"], writes=["GA"])

        def halo(dst, lo, hi, s0, view):
            w = hi - lo
            ts("dve", htmp[:, 0:w], GA[:, 0, lo:hi], sel[:, s0:s0 + 1], None, ALU.mult,
               reads=["GA", "sel"], writes=["htmp"])
            t = htmp[:, 0:w]
            src = GA[:, 1, lo:hi]
            if view:
                t = t.rearrange("p (k x) -> p k x", x=128)
                src = src.rearrange("p (k x) -> p k x", x=128)
            stt("dve", dst, src, sel[:, s0 + 1:s0 + 2], t, ALU.mult, ALU.add,
                reads=["GA", "sel", "htmp"], writes=["Kd", "Va"])

        halo(Kd[:, :, 0:128], 512, 1024, 0, True)
        halo(Kd[:, :, 33 * 128:34 * 128], 0, 512, 2, True)
        halo(Va[:, 0, :], 1600, 2176, 0, False)
        halo(Va[:, 33, :], 1024, 1600, 2, False)
        T.barrier()
        A.reset(pm)
        tw = alloc_tail(li, a_w_out[j], is_last)
        sinkV = A.alloc("sinkV", [2, 16, 128], BF16)
        pat = A.alloc("pat", [2, 2, 128], BF16)
        m2_ = A.mark()
        wst = [A.alloc("wst", [128, 1024], F32) for _ in range(6)]
        load_tail_w(li, tw, a_w_out[j], wst)
        shi32 = A.alloc("shi32", [1, 16, 128], F32)
        slo = A.alloc("slo", [1, 16, 128], BF16)
        src = esink[0:1, 16 * j:16 * j + 16].unsqueeze(2).to_broadcast([1, 16, 128])
        cp("dve", sinkV[0:1, :, :], src, reads=["esink"], writes=["sinkV"])
        cp("dve", shi32, sinkV[0:1, :, :], reads=["sinkV"], writes=["shi32"])
        tt("dve", slo, src, shi32, ALU.subtract, reads=["esink", "shi32"], writes=["slo"])
        dma(sinkV[1:2, :, :], slo[0:1, :, :], reads=["slo", "sinkV"], writes=["sinkV"])
        mset("pool", pat, 0.0, writes=["pat"])
        mset("pool", pat[0:2, 0, 64:128], 1.0, writes=["pat"])
        mset("pool", pat[0:2, 1, 0:64], 1.0, writes=["pat"])
        T.barrier()
        A.reset(m2_)
        Qk = [A.alloc("Qk", [64, 4, TG], BF16) for _ in range(2)]
        Zg1 = A.alloc("Zg", [128, 8, TG], BF16)
        Gs = [A.alloc("G", [128, 8, TG], BF16) for _ in range(2)]
        P = [A.alloc("P", [128, 3, 4, 128], BF16) for _ in range(3)]
        Ob = A.alloc("Ob", [128, 4, 512], F32)
        Rn = A.alloc("Rn", [128, 4, 256], F32)
        dv = distm.rearrange("p (r i) -> p r i", i=128)
        QAh = QA.rearrange("(h d) t -> d h t", d=64)
        v3 = lambda ap: ap.rearrange("p (a b) -> p a b", b=128)
        tb = dict(k=0)

        def nbt():
            tb["k"] = (tb["k"] + 1) % 3
            return 5 + tb["k"]

        its = [(g, kvh, qbl) for g in range(NG) for kvh in range(4) for qbl in range(4)]
        pairs = [(g, kvh) for g in range(NG) for kvh in range(4)]

        def load_q(pi):
            g, kvh = pairs[pi]
            dma(Qk[pi % 2], QAh[:, 4 * kvh:4 * kvh + 4, g * TG:(g + 1) * TG], reads=[("QA", g)],
                writes=[("Qk", pi % 2)])

        def load_z(g):
            dma(Zg1, hview(ZS, g), reads=[("ZS", g)], writes=["Zg"])

        def stageA(n):
            g, kvh, qbl = its[n]
            pi = g * 4 + kvh
            if qbl == 0 and pi + 1 < len(pairs):
                load_q(pi + 1)
            qk = Qk[pi % 2]
            qb = 4 * g + qbl
            for rr in range(3):
                kblk = qb + rr
                mm(ps[:, rr, :].rearrange("p (h q) -> p h q", q=128),
                   Kd[0:64, kvh, kblk * 128:(kblk + 1) * 128],
                   qk[:, :, qbl * 128:(qbl + 1) * 128], True, True,
                   reads=["Kd", ("Qk", pi % 2)], writes=[("ps", rr)])
            for hh in range(4):
                stt("dve", ps[:, 0:3, hh * 128:(hh + 1) * 128], dv, -alibi_slope(4 * kvh + hh),
                    ps[:, 0:3, hh * 128:(hh + 1) * 128], ALU.mult, ALU.add,
                    reads=["distm"] + [("ps", rr) for rr in range(3)],
                    writes=[("ps", rr) for rr in range(3)])
            act(P[n % 3], ps[:, 0:3, :].rearrange("p r (h q) -> p r h q", q=128), AF.Exp,
                reads=[("ps", rr) for rr in range(3)], writes=[("P", n % 3)])

        def stageB(n):
            g, kvh, qbl = its[n]
            qb = 4 * g + qbl
            ob = 3 + n % 2
            Pn = P[n % 3]
            G = Gs[g % 2]
            if kvh == 0 and qbl == 0 and g > 0:
                load_z(g)
            for par in range(2):
                c0 = 64 + 128 * kvh if par == 0 else 128 * kvh
                oreg = ps[:, ob, par * 256:(par + 1) * 256].rearrange("p (h q) -> p h q", q=128)
                for rr in range(3):
                    mm(oreg, Va[:, qb + rr, c0:c0 + 128], Pn[:, rr, par:4:2, :], rr == 0, False,
                       reads=["Va", ("P", n % 3)], writes=[("ps", ob)])
                h0 = 4 * kvh + par
                mm(oreg, pat[0:2, par, :], sinkV[0:2, h0:h0 + 3:2, :], False, True,
                   reads=["sinkV", "pat"], writes=[("ps", ob)])
            cp("dve", Ob[:, qbl, :], ps[:, ob, :], reads=[("ps", ob)], writes=["Ob"])
            if qbl != 3:
                return
            act(Ob[64:128, :, 0:256], Ob[64:128, :, 0:256], AF.Ln, reads=["Ob"], writes=["Ob"])
            act(Ob[0:64, :, 256:512], Ob[0:64, :, 256:512], AF.Ln, reads=["Ob"], writes=["Ob"])
            act(Rn[0:64, :, :], Ob[64:128, :, 0:256], AF.Exp, scale=-1.0, reads=["Ob"],
                writes=["Rn"])
            act(Rn[64:128, :, :], Ob[0:64, :, 256:512], AF.Exp, scale=-1.0, reads=["Ob"],
                writes=["Rn"])
            gt = f"G{g % 2}"
            gw = [(gt, 2 * kvh), (gt, 2 * kvh + 1)]
            for (p0, c0_) in ((0, 0), (64, 256)):
                rz = Rn[p0:p0 + 64, :, :].rearrange("p b (e q) -> p e b q", q=128)
                zz = Zg1[p0:p0 + 64, 2 * kvh:2 * kvh + 2, :].rearrange("p e (b q) -> p e b q", q=128)
                oo_ = Ob[p0:p0 + 64, :, c0_:c0_ + 256].rearrange("p b (e q) -> p e b q", q=128)
                gg = G[p0:p0 + 64, 2 * kvh:2 * kvh + 2, :].rearrange("p e (b q) -> p e b q", q=128)
                tt("pool", rz, rz, zz, ALU.mult, reads=["Rn", "Zg"], writes=["Rn"])
                tt("dve", gg, oo_, rz, ALU.mult, reads=["Ob", "Rn"], writes=gw)

        load_q(0)
        load_z(0)
        stageA(0)
        stageA(1)
        tgen = None
        for n in range(len(its)):
            g, kvh, qbl = its[n]
            if n + 2 < len(its):
                stageA(n + 2)
            stageB(n)
            if tgen is not None:
                try:
                    next(tgen)
                    next(tgen)
                except StopIteration:
                    tgen = None
            if kvh == 3 and qbl == 3:
                if tgen is not None:
                    for _ in tgen:
                        pass
                tgen = tail_gen(li, g, Gs[g % 2], f"G{g % 2}", hsrc, hdst, tw, is_last, nbk=nbt)
        for _ in tgen:
            pass
        T.barrier()

    def layer_B(li, hsrc, hdst, is_last):
        j = li // 2
        SC = float(96.0 ** -0.5)
        A.reset(base_mark)
        pm = A.mark()
        W = A.alloc("Win", [128, 8, 1728], BF16)
        Wq = A.alloc("Wq", [128, 3, 2048], BF16)
        wst = [A.alloc("wst", [128, 1024], F32) for _ in range(6)]
        hTs = [A.alloc("hT", [128, 8, TG], F32) for _ in range(2)]
        sqb = [A.alloc("sq", [128, TG], BF16) for _ in range(2)]
        rs = [A.alloc("r", [128, TG], F32) for _ in range(2)]
        r = A.alloc("r2", [128, TG], F32)
        r3 = A.alloc("r3", [128, TG], F32)
        sqb2 = [A.alloc("sq2", [128, TG], BF16) for _ in range(2)]
        us = [A.alloc("u", [128, 8, TG], BF16) for _ in range(2)]
        cq32 = A.alloc("cq32", [128, 3, TG], F32)
        ckv32 = A.alloc("ckv32", [128, 2, TG], F32)
        cqn = A.alloc("cqn", [128, 3, TG], BF16)
        ckvn = A.alloc("ckvn", [128, 2, TG], BF16)
        Cgs = [A.alloc("Cg", [128, TG], F32) for _ in range(2)]
        Sgs = [A.alloc("Sg", [128, TG], F32) for _ in range(2)]
        t1 = [A.alloc("t1", [128, TG], F32) for _ in range(2)]
        t2 = [A.alloc("t2", [128, TG], F32) for _ in range(2)]
        qst = [A.alloc("qst", [128, TG], BF16) for _ in range(3)]
        zst = [A.alloc("zst", [128, TG], BF16) for _ in range(3)]
        krst = A.alloc("krst", [128, TG], BF16)
        load_w(W, b_w_in[j], 8, 1728, V_NORM + 8 * li, "W", wst)
        load_w(Wq, b_w_qb[j], 3, 2048, V_QN + 3 * j, "Wq", wst)
        def prepB(g):
            par = g % 2
            gs_ = slice(g * TG, (g + 1) * TG)
            dma(hTs[par], hview(hsrc, g), reads=[("hd", g)],
                writes=[(f"h{par}", c) for c in range(8)])
            dma(Cgs[par], ropeC_d[:, gs_], writes=[f"Cg{par}"])
            dma(Sgs[par], ropeS_d[:, gs_], writes=[f"Sg{par}"])

        def prepB_norm(g):
            par = g % 2
            norm_group(hTs[par], f"h{par}", us[par], f"u{par}", 8, rs[par], sqb, TG, D, 32.0,
                       rtok=f"r{par}")
        prepB(0)
        prepB_norm(0)
        for g in range(NG):
            gs = slice(g * TG, (g + 1) * TG)
            if g + 1 < NG:
                prepB(g + 1)
            u = us[g % 2]
            ut = f"u{g % 2}"
            Cg, Sg = Cgs[g % 2], Sgs[g % 2]
            cgt, sgt = f"Cg{g % 2}", f"Sg{g % 2}"

            def proj(co, m, bank):
                for c in range(8):
                    mm(ps[0:m, bank, :], W[:, c, co:co + m], u[:, c, :], c == 0, c == 7,
                       reads=["W", (ut, c)], writes=[("ps", bank)])
            for cc in range(3):
                bank = nb()
                proj(cc * 128, 128, bank)
                act(cq32[:, cc, :], ps[:, bank, :], AF.Copy, reads=[("ps", bank)],
                    writes=[("cq32", cc)])
            for cc in range(2):
                bank = nb()
                proj(384 + cc * 128, 128, bank)
                act(ckv32[:, cc, :], ps[:, bank, :], AF.Copy, reads=[("ps", bank)],
                    writes=[("ckv32", cc)])
            bA = nb()
            proj(640, 32, bA)
            bB = nb()
            proj(672, 32, bB)
            tt("dve", t1[0][0:32, :], ps[0:32, bA, :], Cg[0:32, :], ALU.mult,
               reads=[("ps", bA), cgt], writes=[("t1", 0)])
            tt("dve", t2[0][0:32, :], ps[0:32, bB, :], Sg[0:32, :], ALU.mult,
               reads=[("ps", bB), sgt], writes=[("t2", 0)])
            tt("pool", krst[0:32, :], t1[0][0:32, :], t2[0][0:32, :], ALU.add,
               reads=[("t1", 0), ("t2", 0)], writes=["krst"])
            dma(lat_src[2][:, gs], krst[0:32, :], reads=["krst"], writes=["latsrc"])
            def both_norms():
                yield from norm_gen(cq32, "cq32", cqn, "cqn", 3, r, sqb2, TG, 384,
                                    float(np.sqrt(384.0)), rtok="r2", sqtok="sq2")
                yield from norm_gen(ckv32, "ckv32", ckvn, "ckvn", 2, r3, sqb2, TG, 256, 16.0,
                                    rtok="r3", sqtok="sq2")
            ng = both_norms()
            for oc in range(8):
                bank = nb()
                proj(704 + oc * 128, 128, bank)
                k = uniq() % 3
                act(zst[k], ps[:, bank, :], AF.Silu, reads=[("ps", bank)], writes=[("zst", k)])
                dma(ZS[oc * 128:(oc + 1) * 128, gs], zst[k], reads=[("zst", k)], writes=[("ZS", g)])
                next(ng, None)
                next(ng, None)
            for _ in ng:
                pass
            if g + 1 < NG:
                prepB_norm(g + 1)
            for c in range(2):
                dma(lat_src[c][:, gs], ckvn[:, c, :], reads=[("ckvn", c)], writes=["latsrc"])

            def qproj(co, bank):
                for c in range(3):
                    mm(ps[:, bank, :], Wq[:, c, co:co + 128], cqn[:, c, :], c == 0, c == 2,
                       reads=["Wq", ("cqn", c)], writes=[("ps", bank)])
            for oc in range(8):
                bank = nb()
                qproj(oc * 128, bank)
                k = uniq() % 3
                if oc % 2 == 0:
                    T.add("act", lambda e, o=qst[k], i_=ps[:, bank, :]: e.mul(o, i_, SC),
                          reads=[("ps", bank)], writes=[("qst", k)])
                else:
                    ts("dve", qst[k], ps[:, bank, :], SC, None, ALU.mult, reads=[("ps", bank)],
                       writes=[("qst", k)])
                dma(QN[oc * 128:(oc + 1) * 128, gs], qst[k], reads=[("qst", k)], writes=[("QN", g)])
            for jc in range(4):
                bA = nb()
                qproj(1024 + jc * 128, bA)
                bB = nb()
                qproj(1536 + jc * 128, bB)
                kk = jc % 2
                stt("dve", t1[kk], ps[:, bA, :], SC, Cg, ALU.mult, ALU.mult,
                    reads=[("ps", bA), cgt], writes=[("t1", kk)])
                stt("dve", t2[kk], ps[:, bB, :], SC, Sg, ALU.mult, ALU.mult,
                    reads=[("ps", bB), sgt], writes=[("t2", kk)])
                k = uniq() % 3
                tt("pool", qst[k], t1[kk], t2[kk], ALU.add,
                   reads=[("t1", kk), ("t2", kk)], writes=[("qst", k)])
                dma(QR[jc * 128:(jc + 1) * 128, gs], qst[k], reads=[("qst", k)], writes=[("QR", g)])
        T.barrier()
        for c in range(3):
            T.add("pool", lambda e, c=c: e.collective_compute("AllGather", ALU.bypass, replica_groups=RG,
                                                              ins=[lat_src[c]], outs=[lat_all[c]]),
                  reads=["latsrc"], writes=["latall"], kind="cc")
        A.reset(pm)
        Wkv = A.alloc("Wkv", [128, 2, 2048], BF16)
        wst = [A.alloc("wst", [128, 1024], F32) for _ in range(6)]
        latg = [A.alloc("latg", [128, 2, TG], BF16) for _ in range(2)]
        knst = [A.alloc("knst", [128, 8, TG], BF16) for _ in range(2)]
        Vst = [A.alloc("Vst", [128, 16, 4, 64], BF16) for _ in range(2)]
        load_w(Wkv, b_w_kvb[j], 2, 2048, V_KVN + 2 * j, "Wkv", wst)
        KNv = KN.rearrange("(c p) t -> p c t", p=128)
        for gg in range(16):
            rk, gl = gg // 8, gg % 8
            k = gg % 2
            for c in range(2):
                dma(latg[k][:, c, :], lat_all[c][rk * 128:(rk + 1) * 128, gl * TG:(gl + 1) * TG],
                    reads=["latall"], writes=[("latg", k)])
            for oc in range(8):
                bank = nb()
                for c in range(2):
                    mm(ps[:, bank, :], Wkv[:, c, oc * 128:(oc + 1) * 128], latg[k][:, c, :],
                       c == 0, c == 1, reads=["Wkv", ("latg", k)], writes=[("ps", bank)])
                if oc % 2 == 0:
                    act(knst[k][:, oc, :], ps[:, bank, :], AF.Copy, reads=[("ps", bank)],
                        writes=[("knst", k)])
                else:
                    cp("dve", knst[k][:, oc, :], ps[:, bank, :], reads=[("ps", bank)],
                       writes=[("knst", k)])
            dma(KNv[:, :, gg * TG:(gg + 1) * TG], knst[k], reads=[("knst", k)], writes=["KN"])
            for blk in range(4):
                for half in range(2):
                    bank = nb()
                    for c in range(2):
                        mm(ps[:, bank, :], latg[k][:, c, blk * 128:(blk + 1) * 128],
                           Wkv[:, c, 1024 + half * 512:1536 + half * 512], c == 0, c == 1,
                           reads=["Wkv", ("latg", k)], writes=[("ps", bank)])
                    pv = ps[:, bank, :].rearrange("p (h x) -> p h x", x=64)
                    h0 = half * 8
                    if half == 0:
                        act(Vst[k][:, h0:h0 + 8, blk, :], pv, AF.Copy,
                            reads=[("ps", bank)], writes=[("Vst", k)])
                    else:
                        cp("dve", Vst[k][:, h0:h0 + 8, blk, :], pv,
                           reads=[("ps", bank)], writes=[("Vst", k)])
            dma(VS[:, :, gg * 4:(gg + 1) * 4, :].rearrange("h p k x -> p h k x"), Vst[k],
                reads=[("Vst", k)], writes=["VS"])
        T.barrier()
        A.reset(pm)
        tw = alloc_tail(li, b_w_out[j], is_last)
        m3_ = A.mark()
        wst = [A.alloc("wst", [128, 1024], F32) for _ in range(6)]
        load_tail_w(li, tw, b_w_out[j], wst)
        T.barrier()
        A.reset(m3_)
        Kb = [A.alloc("Kb", [96, SEQ], BF16) for _ in range(2)]
        Vb = [A.alloc("Vb", [128, 64, 128], BF16) for _ in range(2)]
        Qb = [A.alloc("Qb", [96, 1024], BF16) for _ in range(2)]
        Pb = [A.alloc("Pb", [128, 1024], BF16) for _ in range(3)]
        G = A.alloc("G", [128, 8, 1024], BF16)
        Zc = [A.alloc("Zc", [128, 1024], BF16) for _ in range(2)]
        Rn = [A.alloc("Rn", [128, 1024], F32) for _ in range(2)]
        On = [A.alloc("On", [128, 1024], F32) for _ in range(2)]
        mset("pool", Vb[0][:, :, 64:128], 1.0, writes=[("Vb", 0)])
        mset("pool", Vb[1][:, :, 0:64], 1.0, writes=[("Vb", 1)])
        for k in range(2):
            for rk in range(2):
                dma(Kb[k][64:96, rk * TL:(rk + 1) * TL], lat_all[2][rk * 32:(rk + 1) * 32, :],
                    reads=["latall"], writes=[("Kb", k)])

        def load_head(qg, h):
            k = h % 2
            qs = slice(qg * 1024, (qg + 1) * 1024)
            dma(Kb[k][0:64, :], KN[h * 64:(h + 1) * 64, :], reads=["KN"], writes=[("Kb", k)])
            vc = 0 if k == 0 else 64
            dma(Vb[k][:, :, vc:vc + 64], VS[h], reads=["VS"], writes=[("Vb", k)])
            dma(Qb[k][0:64, :], QN[h * 64:(h + 1) * 64, qs], reads=[("QN", gq) for gq in (2 * qg, 2 * qg + 1)],
                writes=[("Qb", k)])
            dma(Qb[k][64:96, :], QR[h * 32:(h + 1) * 32, qs], reads=[("QR", gq) for gq in (2 * qg, 2 * qg + 1)],
                writes=[("Qb", k)])

        Osb = [A.alloc("Osb", [128, 1024], F32) for _ in range(2)]
        OB = 6
        for qg in range(4):
            items = [(h, kt) for h in range(16) for kt in range(64)]

            def smm(n):
                h, kt = items[n]
                k = h % 2
                sb = n % 3
                for hf in range(2):
                    mm(ps[:, 2 * sb + hf, :], Kb[k][0:96, kt * 128:(kt + 1) * 128],
                       Qb[k][0:96, hf * 512:(hf + 1) * 512], True, True,
                       reads=[("Kb", k), ("Qb", k)], writes=[("ps", 2 * sb + hf)])

            load_head(qg, 0)
            load_head(qg, 1)
            dma(Zc[0], ZS[0:128, qg * 1024:(qg + 1) * 1024],
                reads=[("ZS", 2 * qg), ("ZS", 2 * qg + 1)], writes=[("Zc", 0)])
            for n in range(3):
                smm(n)
            for n in range(len(items)):
                h, kt = items[n]
                k = h % 2
                sb = n % 3
                pb = n % 3
                if kt == 0:
                    if 1 <= h and h + 1 < 16:
                        load_head(qg, h + 1)
                    if h % 2 == 0 and h + 2 < 16:
                        pz = h // 2 + 1
                        dma(Zc[pz % 2], ZS[pz * 128:(pz + 1) * 128, qg * 1024:(qg + 1) * 1024],
                            reads=[("ZS", 2 * qg), ("ZS", 2 * qg + 1)], writes=[("Zc", pz % 2)])
                act(Pb[pb].rearrange("p (a b) -> p a b", b=512), ps[:, 2 * sb:2 * sb + 2, :], AF.Exp,
                    reads=[("ps", 2 * sb), ("ps", 2 * sb + 1)], writes=[("Pb", pb)])
                if n + 3 < len(items):
                    smm(n + 3)
                for hf in range(2):
                    mm(ps[:, OB + hf, :], Vb[k][:, kt, :], Pb[pb][:, hf * 512:(hf + 1) * 512],
                       kt == 0, kt == 63, reads=[("Vb", k), ("Pb", pb)], writes=[("ps", OB + hf)])
                if kt == 63:
                    if h % 2 == 0:
                        so, oo = 64, 0
                    else:
                        so, oo = 0, 64
                    ch = h // 2
                    zc = Zc[ch % 2]
                    cp("dve", Osb[k].rearrange("p (a b) -> p a b", b=512), ps[:, OB:OB + 2, :],
                       reads=[("ps", OB), ("ps", OB + 1)], writes=[("Osb", k)])
                    T.add("dve", lambda e, o=Rn[k][oo:oo + 64, :], i_=Osb[k][so:so + 64, :]:
                          e.reciprocal(out=o, in_=i_), reads=[("Osb", k)], writes=[("Rn", k)])
                    tt("dve", On[k][oo:oo + 64, :], Osb[k][oo:oo + 64, :], Rn[k][oo:oo + 64, :], ALU.mult,
                       reads=[("Osb", k), ("Rn", k)], writes=[("On", k)])
                    tt("pool", G[oo:oo + 64, ch, :], On[k][oo:oo + 64, :], zc[oo:oo + 64, :], ALU.mult,
                       reads=[("On", k), ("Zc", ch % 2)], writes=[("G", ch)])
            for s in range(2):
                g = 2 * qg + s
                Gs = G[:, :, s * TG:(s + 1) * TG]
                tail(li, g, Gs, "G", hsrc, hdst, tw, is_last)
        T.barrier()

    n = len(layers)
    for idx, li in enumerate(layers):
        is_last = final and idx == n - 1
        hsrc = hin if idx == 0 else hmid
        hdst = hout if idx == n - 1 else hmid
        if li % 2 == 0:
            layer_A(li, hsrc, hdst, is_last)
        else:
            layer_B(li, hsrc, hdst, is_last)
    T.finish()
    T.emit()
    return nc


def _host_inputs(inp):
    f = np.float32
    x = np.asarray(inp["x"], f)
    p = np.asarray(inp["p"], f)
    a_w_in = np.asarray(inp["a_w_in"], f)
    q, k, v, z = a_w_in[:, :, :1024], a_w_in[:, :, 1024:1280], a_w_in[:, :, 1280:1536], a_w_in[:, :, 1536:]
    kd = np.concatenate([np.concatenate([k[:, :, i * 64:(i + 1) * 64]] * 2, axis=2) for i in range(4)], axis=2)
    a_w_in2 = np.ascontiguousarray(np.concatenate([q, kd, v, z], axis=2))
    b_w_in = np.asarray(inp["b_w_in"], f)
    kr = b_w_in[:, :, 640:672]
    kr_sw = np.concatenate([kr[:, :, 16:], kr[:, :, :16]], axis=2)
    b_w_in2 = np.ascontiguousarray(np.concatenate(
        [b_w_in[:, :, :640], kr, kr_sw, b_w_in[:, :, 672:]], axis=2))
    wq = np.asarray(inp["b_w_qb"], f).reshape(2, 384, 16, 96)
    qn = wq[:, :, :, :64].reshape(2, 384, 1024)
    qr = wq[:, :, :, 64:]
    qr_sw = np.concatenate([qr[..., 16:], qr[..., :16]], axis=-1)
    b_w_qb2 = np.ascontiguousarray(np.concatenate(
        [qn, qr.reshape(2, 384, 512), qr_sw.reshape(2, 384, 512)], axis=2))
    wkv = np.asarray(inp["b_w_kvb"], f).reshape(2, 256, 16, 128)
    b_w_kvb2 = np.ascontiguousarray(np.concatenate(
        [wkv[..., :64].reshape(2, 256, 1024), wkv[..., 64:].reshape(2, 256, 1024)], axis=2))

    def cols(vv, nch):
        return np.asarray(vv, f).reshape(nch, 128).T
    vecs = np.zeros((128, V_N), f)
    for l in range(4):
        vecs[:, V_NORM + 8 * l:V_NORM + 8 * l + 8] = cols(inp["norm_g"][l], 8)
        vecs[:, V_PLE + 8 * l:V_PLE + 8 * l + 8] = cols(inp["ple_norm_g"][l], 8)
    vecs[:, V_FIN:V_FIN + 8] = cols(inp["final_norm_g"], 8)
    for jj in range(2):
        vecs[:, V_QN + 3 * jj:V_QN + 3 * jj + 3] = cols(inp["b_q_norm"][jj], 3)
        vecs[:, V_KVN + 2 * jj:V_KVN + 2 * jj + 2] = cols(inp["b_kv_norm"][jj], 2)
    sinkrep = np.ascontiguousarray(np.broadcast_to(
        np.asarray(inp["a_sink"], f).reshape(1, 32), (128, 32)))
    jj_, rr_, ii_ = np.meshgrid(np.arange(128), np.arange(3), np.arange(128), indexing="ij")
    dist = np.abs((rr_ - 1) * 128 + jj_ - ii_)
    distm = np.where(dist <= 128, dist, 30000).astype(f).reshape(128, 384)
    inv_freq = 10000.0 ** (-np.arange(16, dtype=np.float64) / 16.0)
    shared = dict(a_w_in2=a_w_in2, a_w_out=np.asarray(inp["a_w_out"], f), b_w_in2=b_w_in2,
                  b_w_qb2=b_w_qb2, b_w_kvb2=b_w_kvb2, b_w_out=np.asarray(inp["b_w_out"], f),
                  ple_w=np.asarray(inp["ple_w"], f), ple_w_gate=np.asarray(inp["ple_w_gate"], f),
                  vecs=vecs, sinkrep=sinkrep, distm=distm)
    per_core = []
    for c in range(NCORES):
        b, hf = c // 2, c % 2
        sl = slice(hf * TL, (hf + 1) * TL)
        pos = np.arange(hf * TL, (hf + 1) * TL, dtype=np.float64)
        ang = pos[None, :] * inv_freq[:, None]
        cc = np.cos(ang)
        ss = np.sin(ang)
        C32 = np.concatenate([cc, cc], axis=0)
        S32 = np.concatenate([-ss, ss], axis=0)
        d = dict(shared)
        d["ropeC"] = np.ascontiguousarray(np.tile(C32, (4, 1)).astype(f))
        d["ropeS"] = np.ascontiguousarray(np.tile(S32, (4, 1)).astype(f))
        selv = [0, 0, 0, 1] if hf == 0 else [1, 0, 0, 0]
        d["sel"] = np.ascontiguousarray(np.broadcast_to(np.asarray(selv, f)[None, :], (128, 4)))
        d["hin"] = np.ascontiguousarray(x[b, sl, :].T)
        d["pT"] = np.ascontiguousarray(p[:, b, sl, :].transpose(0, 2, 1))
        per_core.append(d)
    return per_core


_NC_CACHE = {}


def _get_nc(layers, final):
    key = (tuple(layers), final)
    if key not in _NC_CACHE:
        _NC_CACHE[key] = build_program(list(layers), final)
    return _NC_CACHE[key]


FUSED = True


def kernel(**inputs):
    per_core = _host_inputs(inputs)
    if FUSED:
        plan = [([0, 1, 2, 3], True)]
    else:
        plan = [([0], False), ([1], False), ([2], False), ([3], True)]
    for layers, final in plan:
        nc = _get_nc(layers, final)
        res = run_bass_kernel_spmd(nc, per_core, core_ids=list(range(NCORES)))
        for c in range(NCORES):
            per_core[c]["hin"] = np.asarray(res.results[c]["hout"], np.float32)
    out = np.empty((4, SEQ, D), np.float32)
    for c in range(NCORES):
        b, hf = c // 2, c % 2
        out[b, hf * TL:(hf + 1) * TL, :] = per_core[c]["hin"].T
    return out
```

```python
import numpy as np
import concourse.bass as bass
import concourse.mybir as mybir
from concourse.bass_utils import run_bass_kernel_spmd

F32 = mybir.dt.float32
BF16 = mybir.dt.bfloat16
AF = mybir.ActivationFunctionType
ALU = mybir.AluOpType

NCORES = 8
D = 1024
TL = 4096
SEQ = 8192
TG = 512
NG = TL // TG
EPS = 1e-6
SB_LO = 16640
SB_HI = 229376

V_NORM = 0
V_PLE = 32
V_FIN = 64
V_QN = 72
V_KVN = 78
V_N = 82


class Tracker:
    ENGS = ["sp", "act", "dve", "pool", "pe"]

    def __init__(self, nc, n_dma_sems=24):
        self.nc = nc
        self.I = []
        self.lw = {}
        self.rd = {}
        self.nds = n_dma_sems
        self.dma_count = 0
        self.dma_last = {}
        self.cc_count = 0
        self.cc_last = None
        self.last_stream = {}
        self.bar = {e: {} for e in self.ENGS}

    def _stream(self, ins):
        if ins["kind"] == "c":
            return ins["eng"]
        if ins["kind"] == "dma":
            return ("dma", ins["slot"])
        return ("cc",)

    MAXI = None

    def add(self, eng, fn, reads=(), writes=(), kind="c"):
        i = len(self.I)
        if Tracker.MAXI is not None and i >= Tracker.MAXI and fn is not None:
            return None
        ins = dict(eng=eng, fn=fn, kind=kind, sig=False)
        deps = {}

        def dep(j):
            s = self._stream(self.I[j])
            if deps.get(s, -1) < j:
                deps[s] = j

        if kind == "dma":
            slot = self.dma_count % self.nds
            ins["slot"] = slot
            ins["target"] = 16 * (self.dma_count // self.nds + 1)
            self.dma_count += 1
            if slot in self.dma_last:
                dep(self.dma_last[slot])
            self.dma_last[slot] = i
        elif kind == "cc":
            self.cc_count += 1
            ins["target"] = self.cc_count
            if self.cc_last is not None:
                dep(self.cc_last)
            self.cc_last = i
        me = self._stream(ins)

        def same_c(j):
            o = self.I[j]
            return kind == "c" and o["kind"] == "c" and o["eng"] == eng

        for t in reads:
            for s_, w in self.lw.get(t, {}).items():
                if same_c(w) and eng == "pe":
                    continue
                dep(w)
        for t in writes:
            for s_, w in self.lw.get(t, {}).items():
                if not (same_c(w) and eng == "pe"):
                    dep(w)
            for s_, r in self.rd.get(t, {}).items():
                dep(r)
        for s_, j in self.bar[eng].items():
            if deps.get(s_, -1) < j:
                deps[s_] = j
        self.bar[eng] = {}
        ins["deps"] = deps
        self.I.append(ins)
        for t in reads:
            self.rd.setdefault(t, {})[me] = i
        for t in writes:
            self.lw.setdefault(t, {})[me] = i
        self.last_stream[me] = i
        return i

    def barrier(self):
        for e in self.ENGS:
            for s, j in self.last_stream.items():
                if self.bar[e].get(s, -1) < j:
                    self.bar[e][s] = j

    def finish(self):
        self.barrier()
        self.add("sp", None)

    def emit(self):
        nc = self.nc
        for ins in self.I:
            for s, j in ins["deps"].items():
                self.I[j]["sig"] = True
        cnt = {e: 0 for e in self.ENGS}
        for ins in self.I:
            if ins["kind"] == "c" and ins["sig"]:
                cnt[ins["eng"]] += 1
                ins["tick"] = cnt[ins["eng"]]
        csem = {e: nc.alloc_semaphore(f"c_{e}") for e in ["act", "dve", "pool", "pe"]}
        dsem = [nc.alloc_semaphore(f"d_{k}") for k in range(self.nds)]
        ccsem = nc.alloc_semaphore("ccsem")
        I = self.I

        def run(engname, e):
            waited = {}
            for ins in I:
                if ins["eng"] != engname:
                    continue
                for s, j in ins["deps"].items():
                    p = I[j]
                    if p["kind"] == "c":
                        sem, val = csem[p["eng"]], p["tick"]
                    elif p["kind"] == "dma":
                        sem, val = dsem[p["slot"]], p["target"]
                    else:
                        sem, val = ccsem, p["target"]
                    if waited.get(s, 0) >= val:
                        continue
                    waited[s] = val
                    e.wait_ge(sem, val)
                if ins["fn"] is None:
                    continue
                bi = ins["fn"](e)
                if ins["kind"] == "dma":
                    bi.then_inc(dsem[ins["slot"]], 16)
                elif ins["kind"] == "cc":
                    bi.then_inc(ccsem, 1)
                elif ins["sig"]:
                    bi.then_inc(csem[engname], 1)

        with nc.Block() as block:
            @block.sync
            def _(e):
                run("sp", e)

            @block.scalar
            def _(e):
                run("act", e)

            @block.vector
            def _(e):
                run("dve", e)

            @block.gpsimd
            def _(e):
                run("pool", e)

            @block.tensor
            def _(e):
                run("pe", e)


class Arena:
    def __init__(self, nc):
        self.nc = nc
        self.cur = SB_LO
        self.uid = 0

    def alloc(self, name, shape, dtype):
        n = 1
        for s in shape[1:]:
            n *= s
        nb = n * (4 if dtype == F32 else 2)
        nb = (nb + 31) // 32 * 32
        off = self.cur
        assert off + nb <= SB_HI, f"SBUF overflow at {name}: {off + nb}"
        self.cur = off + nb
        self.uid += 1
        h = self.nc.alloc_sbuf_tensor_at(f"{name}_{self.uid}", list(shape), dtype, offset=off)
        return h.ap()

    def mark(self):
        return self.cur

    def reset(self, m):
        self.cur = m


def alibi_slope(h):
    return float(2.0 ** (-8.0 * (h + 1) / 16.0))


def build_program(layers, final, ncores=NCORES):
    nc = bass.Bass("TRN2", target_bir_lowering=False)
    T = Tracker(nc)
    A = Arena(nc)

    def din(name, shape, dt=F32):
        return nc.dram_tensor(name, list(shape), dt, kind="ExternalInput").ap()

    hin = din("hin", [D, TL])
    pT = din("pT", [4, 256, TL])
    vecs_d = din("vecs", [128, V_N])
    sink_d = din("sinkrep", [128, 32])
    distm_d = din("distm", [128, 384])
    sel_d = din("sel", [128, 4])
    ropeC_d = din("ropeC", [128, TL])
    ropeS_d = din("ropeS", [128, TL])
    a_w_in = din("a_w_in2", [2, D, 2816])
    a_w_out = din("a_w_out", [2, D, D])
    b_w_in = din("b_w_in2", [2, D, 1728])
    b_w_qb = din("b_w_qb2", [2, 384, 2048])
    b_w_kvb = din("b_w_kvb2", [2, 256, 2048])
    b_w_out = din("b_w_out", [2, D, D])
    ple_w = din("ple_w", [4, 256, D])
    ple_wg = din("ple_w_gate", [4, D, D])
    hout = nc.dram_tensor("hout", [D, TL], F32, kind="ExternalOutput").ap()

    hmid = nc.dram_tensor("hmid", [D, TL], F32).ap()
    QA = nc.dram_tensor("QA", [D, TL], BF16).ap()
    ZS = nc.dram_tensor("ZS", [D, TL], BF16).ap()
    AGW = 2176
    agA_src = nc.dram_tensor("agA_src", [128, AGW], BF16).ap()
    agA_dst = nc.dram_tensor("agA_dst", [256, AGW], BF16).ap()
    QN = nc.dram_tensor("QN", [D, TL], BF16).ap()
    QR = nc.dram_tensor("QR", [512, TL], BF16).ap()
    lat_src = [nc.dram_tensor(f"lat_src{c}", [128 if c < 2 else 32, TL], BF16).ap() for c in range(3)]
    lat_all = [nc.dram_tensor(f"lat_all{c}", [256 if c < 2 else 64, TL], BF16).ap() for c in range(3)]
    KN = nc.dram_tensor("KN", [D, SEQ], BF16).ap()
    VS = nc.dram_tensor("VS", [16, 128, 64, 64], BF16).ap()

    ps = nc.alloc_psum_tensor("ps", [128, 8, 512], F32).ap()
    RG = [[2 * i, 2 * i + 1] for i in range(ncores // 2)]

    state = dict(bank=0, k=0)

    def nb():
        b = state["bank"]
        state["bank"] = (b + 1) % 8
        return b

    def uniq():
        state["k"] += 1
        return state["k"]

    def dma(out, in_, reads=(), writes=()):
        T.add("sp", lambda e, out=out, in_=in_: e.dma_start(out=out, in_=in_),
              reads=reads, writes=writes, kind="dma")

    def mm(out, lhsT, rhs, start, stop, reads=(), writes=()):
        T.add("pe", lambda e, out=out, lhsT=lhsT, rhs=rhs, start=start, stop=stop:
              e.matmul(out, lhsT, rhs, start=start, stop=stop), reads=reads, writes=writes)

    def act(out, in_, func, reads=(), writes=(), scale=1.0):
        T.add("act", lambda e, out=out, in_=in_, func=func, scale=scale:
              e.activation(out=out, in_=in_, func=func, scale=scale), reads=reads, writes=writes)

    def tt(eng, out, in0, in1, op, reads=(), writes=()):
        T.add(eng, lambda e, out=out, in0=in0, in1=in1, op=op:
              e.tensor_tensor(out=out, in0=in0, in1=in1, op=op), reads=reads, writes=writes)

    def ts(eng, out, in0, s1, s2, op0, op1=None, reads=(), writes=()):
        if op1 is None:
            T.add(eng, lambda e, out=out, in0=in0, s1=s1, op0=op0:
                  e.tensor_scalar(out=out, in0=in0, scalar1=s1, scalar2=None, op0=op0),
                  reads=reads, writes=writes)
        else:
            T.add(eng, lambda e, out=out, in0=in0, s1=s1, s2=s2, op0=op0, op1=op1:
                  e.tensor_scalar(out=out, in0=in0, scalar1=s1, scalar2=s2, op0=op0, op1=op1),
                  reads=reads, writes=writes)

    def stt(eng, out, in0, scalar, in1, op0, op1, reads=(), writes=()):
        T.add(eng, lambda e, out=out, in0=in0, scalar=scalar, in1=in1, op0=op0, op1=op1:
              e.scalar_tensor_tensor(out=out, in0=in0, scalar=scalar, in1=in1, op0=op0, op1=op1),
              reads=reads, writes=writes)

    def cp(eng, out, in_, reads=(), writes=()):
        T.add(eng, lambda e, out=out, in_=in_: e.tensor_copy(out=out, in_=in_),
              reads=reads, writes=writes)

    def mset(eng, ap, val, writes=()):
        T.add(eng, lambda e, ap=ap, val=val: e.memset(ap, val), writes=writes)

    ones = A.alloc("ones", [128, 128], BF16)
    vecs = A.alloc("vecs", [128, V_N], F32)
    gf32 = A.alloc("gf32", [128, 8], F32)
    sinkrep = A.alloc("sinkrep", [128, 32], F32)
    esink = A.alloc("esink", [128, 32], F32)
    distm = A.alloc("distm", [128, 384], F32)
    sel = A.alloc("sel", [128, 4], F32)
    mset("pool", ones, 1.0, writes=["ones"])
    dma(vecs, vecs_d, writes=["vecs"])
    dma(sinkrep, sink_d, writes=["sinkrep"])
    dma(distm, distm_d, writes=["distm"])
    dma(sel, sel_d, writes=["sel"])
    ts("pool", gf32, vecs[:, V_FIN:V_FIN + 8], 1.0, None, ALU.mult, reads=["vecs"], writes=["gf32"])
    act(esink, sinkrep, AF.Exp, reads=["sinkrep"], writes=["esink"])
    base_mark = A.mark()

    def load_w(dst, src, KC, N, gcol, tok, wst):
        PW = 1024
        pieces = [(n0, min(n0 + PW, N)) for n0 in range(0, N, PW)]
        for c in range(KC):
            for (n0, n1) in pieces:
                k = uniq() % len(wst)
                w = n1 - n0
                dma(wst[k][:, 0:w], src[c * 128:(c + 1) * 128, n0:n1], writes=[("wst", k)])
                s1 = vecs[:, gcol + c:gcol + c + 1] if gcol is not None else 1.0
                if k % 2 == 0:
                    ts("dve", dst[:, c, n0:n1], wst[k][:, 0:w], s1, None, ALU.mult,
                       reads=[("wst", k), "vecs"], writes=[tok])
                else:
                    act(dst[:, c, n0:n1], wst[k][:, 0:w], AF.Copy, scale=s1,
                        reads=[("wst", k), "vecs"], writes=[tok])

    def norm_gen(src, stok, dst, dtok, nch, r, sqb, wts, dim, mult, rtok="r", nbk=None, sqtok="sq"):
        bank = (nbk or nb)()
        pend = []
        for c in range(nch):
            k = c % 2
            eng = "pool" if c % 2 == 0 else "dve"
            tt(eng, sqb[k], src[:, c, :], src[:, c, :], ALU.mult,
               reads=[(stok, c)], writes=[(sqtok, k)])
            pend.append((c, k))
            if len(pend) == 2 or c == nch - 1:
                yield
                for (c_, k_) in pend:
                    mm(ps[:, bank, 0:wts], ones, sqb[k_][:, 0:wts], c_ == 0, c_ == nch - 1,
                       reads=[(sqtok, k_), "ones"], writes=[("ps", bank)])
                pend = []
        yield
        m2 = float(mult) ** 2
        ts("dve", r, ps[:, bank, 0:wts], 1.0 / m2, float(dim * EPS) / m2, ALU.mult, ALU.add,
           reads=[("ps", bank)], writes=[rtok])
        yield
        act(r, r, AF.Ln, reads=[rtok], writes=[rtok])
        act(r, r, AF.Exp, scale=-0.5, reads=[rtok], writes=[rtok])
        yield
        for c in range(nch):
            eng = "dve" if c % 2 == 0 else "pool"
            tt(eng, dst[:, c, :], src[:, c, :], r, ALU.mult,
               reads=[(stok, c), rtok], writes=[(dtok, c)])
            if c % 4 == 3:
                yield

    def norm_group(*a, **kw):
        for _ in norm_gen(*a, **kw):
            pass

    def hview(ap, g):
        return ap.rearrange("(c p) t -> p c t", p=128)[:, :, g * TG:(g + 1) * TG]

    def tail_gen(li, g, G, gtok, hsrc, hdst, tw, is_last, nbk=None):
        nbk = nbk or nb
        hT, sqb, r, n2, pst, pbf, gate, tmpb, obuf = (tw[k] for k in
            ["hT", "sqb", "r", "n2", "pst", "pbf", "gate", "tmp", "obuf"])
        Wo, Wg, Wp = tw["Wo"], tw["Wg"], tw["Wp"]
        dma(hT, hview(hsrc, g), reads=[("hd", g)], writes=[("h", c) for c in range(8)])
        dma(pst, pT[li].rearrange("(c p) t -> p c t", p=128)[:, :, g * TG:(g + 1) * TG],
            writes=["pst"])
        yield
        cp("pool", pbf, pst, reads=["pst"], writes=["pbf"])
        banks = {}
        for oc in range(9):
            if oc < 8:
                bank = nbk()
                banks[oc] = bank
                for c in range(8):
                    mm(ps[:, bank, :], Wo[:, c, oc * 128:(oc + 1) * 128], G[:, c, :], c == 0, c == 7,
                       reads=["Wo", (gtok, c)], writes=[("ps", bank)])
            if oc >= 1:
                o = oc - 1
                tt("dve", hT[:, o, :], ps[:, banks[o], :], hT[:, o, :], ALU.add,
                   reads=[("ps", banks[o]), ("h", o)], writes=[("h", o)])
            yield
        for _ in norm_gen(hT, "h", n2, "n2", 8, r, sqb, TG, D, 32.0, nbk=nbk):
            yield
        b1s, b2s = {}, {}
        for oc in range(9):
            if oc < 8:
                b1 = nbk()
                b1s[oc] = b1
                for c in range(8):
                    mm(ps[:, b1, :], Wg[:, c, oc * 128:(oc + 1) * 128], n2[:, c, :], c == 0, c == 7,
                       reads=["Wg", ("n2", c)], writes=[("ps", b1)])
            if 1 <= oc < 9:
                o = oc - 1
                act(gate[0], ps[:, b1s[o], :], AF.Sigmoid, reads=[("ps", b1s[o])], writes=[("gate", 0)])
                b2 = nbk()
                b2s[o] = b2
                for c in range(2):
                    mm(ps[:, b2, :], Wp[:, c, o * 128:(o + 1) * 128], pbf[:, c, :], c == 0, c == 1,
                       reads=["Wp", "pbf"], writes=[("ps", b2)])
                tt("dve", tmpb[0], ps[:, b2, :], gate[0], ALU.mult,
                   reads=[("ps", b2), ("gate", 0)], writes=[("tmp", 0)])
                tt("pool", hT[:, o, :], hT[:, o, :], tmpb[0], ALU.add,
                   reads=[("h", o), ("tmp", 0)], writes=[("h", o)])
            yield
        if not is_last:
            dma(hview(hdst, g), hT, reads=[("h", c) for c in range(8)], writes=[("hd", g)])
        else:
            bank = nbk()
            for c in range(8):
                k = c % 2
                tt("pool" if c % 2 == 0 else "dve", sqb[k], hT[:, c, :], hT[:, c, :], ALU.mult,
                   reads=[("h", c)], writes=[("sq", k)])
                mm(ps[:, bank, :], ones, sqb[k], c == 0, c == 7,
                   reads=[("sq", k), "ones"], writes=[("ps", bank)])
            ts("dve", r, ps[:, bank, :], 1.0 / 1024.0, float(EPS), ALU.mult, ALU.add,
               reads=[("ps", bank)], writes=["r"])
            act(r, r, AF.Ln, reads=["r"], writes=["r"])
            act(r, r, AF.Exp, scale=-0.5, reads=["r"], writes=["r"])
            for c in range(8):
                k = c % 2
                stt("dve", obuf[k], hT[:, c, :], gf32[:, c:c + 1], r, ALU.mult, ALU.mult,
                    reads=[("h", c), "r", "gf32"], writes=[("ob", k)])
                dma(hout[c * 128:(c + 1) * 128, g * TG:(g + 1) * TG], obuf[k],
                    reads=[("ob", k)], writes=[("hd", g)])

    def tail(*a, **kw):
        for _ in tail_gen(*a, **kw):
            pass

    def alloc_tail(li, wo_src, is_last):
        tw = {}
        tw["Wo"] = A.alloc("Wo", [128, 8, D], BF16)
        tw["Wg"] = A.alloc("Wg", [128, 8, D], BF16)
        tw["Wp"] = A.alloc("Wp", [128, 2, D], BF16)
        tw["hT"] = A.alloc("hT", [128, 8, TG], F32)
        tw["sqb"] = [A.alloc("sq", [128, TG], BF16) for _ in range(2)]
        tw["r"] = A.alloc("r", [128, TG], F32)
        tw["n2"] = A.alloc("n2", [128, 8, TG], BF16)
        tw["pst"] = A.alloc("pst", [128, 2, TG], F32)
        tw["pbf"] = A.alloc("pbf", [128, 2, TG], BF16)
        tw["gate"] = [A.alloc("gate", [128, TG], F32)] * 2
        tw["tmp"] = [A.alloc("tmp", [128, TG], F32)] * 2
        tw["obuf"] = [A.alloc("obuf", [128, TG], F32) for _ in range(2)] if is_last else None
        return tw

    def load_tail_w(li, tw, wo_src, wst):
        load_w(tw["Wo"], wo_src, 8, D, None, "Wo", wst)
        load_w(tw["Wg"], ple_wg[li], 8, D, V_PLE + 8 * li, "Wg", wst)
        load_w(tw["Wp"], ple_w[li], 2, D, None, "Wp", wst)

    def layer_A(li, hsrc, hdst, is_last):
        j = li // 2
        A.reset(base_mark)
        Kd = A.alloc("Kd", [128, 4, 34 * 128], BF16)
        Va = A.alloc("Va", [128, 34, 576], BF16)
        pm = A.mark()
        W = A.alloc("Win", [128, 8, 2816], BF16)
        wst = [A.alloc("wst", [128, 1024], F32) for _ in range(6)]
        hTs = [A.alloc("hT", [128, 8, TG], F32) for _ in range(2)]
        sqb = [A.alloc("sq", [128, TG], BF16) for _ in range(2)]
        rs = [A.alloc("r", [128, TG], F32) for _ in range(2)]
        us = [A.alloc("u", [128, 8, TG], BF16) for _ in range(2)]
        qst = [A.alloc("qst", [128, TG], BF16) for _ in range(3)]
        zst = [A.alloc("zst", [128, TG], BF16) for _ in range(3)]
        mset("pool", Va, 1.0, writes=["Va"])
        load_w(W, a_w_in[j], 8, 2816, V_NORM + 8 * li, "W", wst)

        def prep_load(g):
            par = g % 2
            dma(hTs[par], hview(hsrc, g), reads=[("hd", g)],
                writes=[(f"h{par}", c) for c in range(8)])

        def prep_norm(g):
            par = g % 2
            norm_group(hTs[par], f"h{par}", us[par], f"u{par}", 8, rs[par], sqb, TG, D, 32.0,
                       rtok=f"r{par}")
        prep_load(0)
        prep_norm(0)
        for g in range(NG):
            if g + 1 < NG:
                prep_load(g + 1)
            u = us[g % 2]
            ut = f"u{g % 2}"
            for oc in range(8):
                bank = nb()
                for c in range(8):
                    mm(ps[:, bank, :], W[:, c, oc * 128:(oc + 1) * 128], u[:, c, :], c == 0, c == 7,
                       reads=["W", (ut, c)], writes=[("ps", bank)])
                k = uniq() % 3
                T.add("act", lambda e, o=qst[k], i_=ps[:, bank, :]: e.mul(o, i_, 0.125),
                      reads=[("ps", bank)], writes=[("qst", k)])
                dma(QA[oc * 128:(oc + 1) * 128, g * TG:(g + 1) * TG], qst[k],
                    reads=[("qst", k)], writes=[("QA", g)])
            if g + 1 < NG:
                prep_norm(g + 1)
            for kvh in range(4):
                bank = nb()
                co = 1024 + kvh * 128
                for c in range(8):
                    mm(ps[:, bank, :], W[:, c, co:co + 128], u[:, c, :], c == 0, c == 7,
                       reads=["W", (ut, c)], writes=[("ps", bank)])
                cp("dve", Kd[:, kvh, (1 + 4 * g) * 128:(5 + 4 * g) * 128], ps[:, bank, :],
                   reads=[("ps", bank)], writes=["Kd"])
            for blk in range(4):
                bank = nb()
                for c in range(8):
                    mm(ps[:, bank, 0:256], u[:, c, blk * 128:(blk + 1) * 128], W[:, c, 1536:1792],
                       c == 0, c == 7, reads=["W", (ut, c)], writes=[("ps", bank)])
                dst = Va[:, 1 + 4 * g + blk, 64:576].rearrange("p (k x) -> p k x", x=128)[:, :, 0:64]
                cp("dve", dst, ps[:, bank, 0:256].rearrange("p (k x) -> p k x", x=64),
                   reads=[("ps", bank)], writes=["Va"])
            for oc in range(8):
                bank = nb()
                co = 1792 + oc * 128
                for c in range(8):
                    mm(ps[:, bank, :], W[:, c, co:co + 128], u[:, c, :], c == 0, c == 7,
                       reads=["W", (ut, c)], writes=[("ps", bank)])
                k = uniq() % 3
                act(zst[k], ps[:, bank, :], AF.Silu, reads=[("ps", bank)], writes=[("zst", k)])
                dma(ZS[oc * 128:(oc + 1) * 128, g * TG:(g + 1) * TG], zst[k],
                    reads=[("zst", k)], writes=[("ZS", g)])
        dma(agA_src[:, 0:512].rearrange("p (k x) -> p k x", x=128), Kd[:, :, 128:256],
            reads=["Kd"], writes=["agsrc"])
        dma(agA_src[:, 512:1024].rearrange("p (k x) -> p k x", x=128), Kd[:, :, 32 * 128:33 * 128],
            reads=["Kd"], writes=["agsrc"])
        dma(agA_src[:, 1024:1600], Va[:, 1, :], reads=["Va"], writes=["agsrc"])
        dma(agA_src[:, 1600:2176], Va[:, 32, :], reads=["Va"], writes=["agsrc"])
        T.barrier()
        T.add("pool", lambda e: e.collective_compute("AllGather", ALU.bypass, replica_groups=RG,
                                                     ins=[agA_src], outs=[agA_dst]),
              reads=["agsrc"], writes=["agdst"], kind="cc")
        A.reset(pm)
        GA = A.alloc("GA", [128, 2, AGW], BF16)
        htmp = A.alloc("htmp", [128, 576], BF16)
        dma(GA, agA_dst.rearrange("(r p) w -> p r w", p=128), reads=["agdst"], writes=["GA"])

        def halo(dst, lo, hi, s0, view):
            w = hi - lo
            ts("dve", htmp[:, 0:w], GA[:, 0, lo:hi], sel[:, s0:s0 + 1], None, ALU.mult,
               reads=["GA", "sel"], writes=["htmp"])
            t = htmp[:, 0:w]
            src = GA[:, 1, lo:hi]
            if view:
                t = t.rearrange("p (k x) -> p k x", x=128)
                src = src.rearrange("p (k x) -> p k x", x=128)
            stt("dve", dst, src, sel[:, s0 + 1:s0 + 2], t, ALU.mult, ALU.add,
                reads=["GA", "sel", "htmp"], writes=["Kd", "Va"])

        halo(Kd[:, :, 0:128], 512, 1024, 0, True)
        halo(Kd[:, :, 33 * 128:34 * 128], 0, 512, 2, True)
        halo(Va[:, 0, :], 1600, 2176, 0, False)
        halo(Va[:, 33, :], 1024, 1600, 2, False)
        T.barrier()
        A.reset(pm)
        tw = alloc_tail(li, a_w_out[j], is_last)
        sinkV = A.alloc("sinkV", [2, 16, 128], BF16)
        pat = A.alloc("pat", [2, 2, 128], BF16)
        m2_ = A.mark()
        wst = [A.alloc("wst", [128, 1024], F32) for _ in range(6)]
        load_tail_w(li, tw, a_w_out[j], wst)
        shi32 = A.alloc("shi32", [1, 16, 128], F32)
        slo = A.alloc("slo", [1, 16, 128], BF16)
        src = esink[0:1, 16 * j:16 * j + 16].unsqueeze(2).to_broadcast([1, 16, 128])
        cp("dve", sinkV[0:1, :, :], src, reads=["esink"], writes=["sinkV"])
        cp("dve", shi32, sinkV[0:1, :, :], reads=["sinkV"], writes=["shi32"])
        tt("dve", slo, src, shi32, ALU.subtract, reads=["esink", "shi32"], writes=["slo"])
        dma(sinkV[1:2, :, :], slo[0:1, :, :], reads=["slo", "sinkV"], writes=["sinkV"])
        mset("pool", pat, 0.0, writes=["pat"])
        mset("pool", pat[0:2, 0, 64:128], 1.0, writes=["pat"])
        mset("pool", pat[0:2, 1, 0:64], 1.0, writes=["pat"])
        T.barrier()
        A.reset(m2_)
        Qk = [A.alloc("Qk", [64, 4, TG], BF16) for _ in range(2)]
        Zg1 = A.alloc("Zg", [128, 8, TG], BF16)
        Gs = [A.alloc("G", [128, 8, TG], BF16) for _ in range(2)]
        P = [A.alloc("P", [128, 3, 4, 128], BF16) for _ in range(3)]
        Ob = A.alloc("Ob", [128, 4, 512], F32)
        Rn = A.alloc("Rn", [128, 4, 256], F32)
        dv = distm.rearrange("p (r i) -> p r i", i=128)
        QAh = QA.rearrange("(h d) t -> d h t", d=64)
        v3 = lambda ap: ap.rearrange("p (a b) -> p a b", b=128)
        tb = dict(k=0)

        def nbt():
            tb["k"] = (tb["k"] + 1) % 3
            return 5 + tb["k"]

        its = [(g, kvh, qbl) for g in range(NG) for kvh in range(4) for qbl in range(4)]
        pairs = [(g, kvh) for g in range(NG) for kvh in range(4)]

        def load_q(pi):
            g, kvh = pairs[pi]
            dma(Qk[pi % 2], QAh[:, 4 * kvh:4 * kvh + 4, g * TG:(g + 1) * TG], reads=[("QA", g)],
                writes=[("Qk", pi % 2)])

        def load_z(g):
            dma(Zg1, hview(ZS, g), reads=[("ZS", g)], writes=["Zg"])

        def stageA(n):
            g, kvh, qbl = its[n]
            pi = g * 4 + kvh
            if qbl == 0 and pi + 1 < len(pairs):
                load_q(pi + 1)
            qk = Qk[pi % 2]
            qb = 4 * g + qbl
            s0 = 3 * (n % 2)
            sbk = [("ps", s0 + rr) for rr in range(3)]
            for rr in range(3):
                kblk = qb + rr
                mm(ps[:, s0 + rr, :].rearrange("p (h q) -> p h q", q=128),
                   Kd[0:64, kvh, kblk * 128:(kblk + 1) * 128],
                   qk[:, :, qbl * 128:(qbl + 1) * 128], True, True,
                   reads=["Kd", ("Qk", pi % 2)], writes=[("ps", s0 + rr)])
            for hh in range(4):
                stt("dve", ps[:, s0:s0 + 3, hh * 128:(hh + 1) * 128], dv, -alibi_slope(4 * kvh + hh),
                    ps[:, s0:s0 + 3, hh * 128:(hh + 1) * 128], ALU.mult, ALU.add,
                    reads=["distm"] + sbk, writes=sbk)
            act(P[n % 3], ps[:, s0:s0 + 3, :].rearrange("p r (h q) -> p r h q", q=128), AF.Exp,
                reads=sbk, writes=[("P", n % 3)])

        def stageB(n):
            g, kvh, qbl = its[n]
            qb = 4 * g + qbl
            ob = 6 + n % 2
            Pn = P[n % 3]
            G = Gs[g % 2]
            if kvh == 0 and qbl == 0 and g > 0:
                load_z(g)
            for par in range(2):
                c0 = 64 + 128 * kvh if par == 0 else 128 * kvh
                oreg = ps[:, ob, par * 256:(par + 1) * 256].rearrange("p (h q) -> p h q", q=128)
                for rr in range(3):
                    mm(oreg, Va[:, qb + rr, c0:c0 + 128], Pn[:, rr, par:4:2, :], rr == 0, False,
                       reads=["Va", ("P", n % 3)], writes=[("ps", ob)])
                h0 = 4 * kvh + par
                mm(oreg, pat[0:2, par, :], sinkV[0:2, h0:h0 + 3:2, :], False, True,
                   reads=["sinkV", "pat"], writes=[("ps", ob)])
            cp("dve", Ob[:, qbl, :], ps[:, ob, :], reads=[("ps", ob)], writes=["Ob"])
            if qbl != 3:
                return
            act(Ob[64:128, :, 0:256], Ob[64:128, :, 0:256], AF.Ln, reads=["Ob"], writes=["Ob"])
            act(Ob[0:64, :, 256:512], Ob[0:64, :, 256:512], AF.Ln, reads=["Ob"], writes=["Ob"])
            act(Rn[0:64, :, :], Ob[64:128, :, 0:256], AF.Exp, scale=-1.0, reads=["Ob"],
                writes=["Rn"])
            act(Rn[64:128, :, :], Ob[0:64, :, 256:512], AF.Exp, scale=-1.0, reads=["Ob"],
                writes=["Rn"])
            gt = f"G{g % 2}"
            gw = [(gt, 2 * kvh), (gt, 2 * kvh + 1)]
            for (p0, c0_) in ((0, 0), (64, 256)):
                rz = Rn[p0:p0 + 64, :, :].rearrange("p b (e q) -> p e b q", q=128)
                zz = Zg1[p0:p0 + 64, 2 * kvh:2 * kvh + 2, :].rearrange("p e (b q) -> p e b q", q=128)
                oo_ = Ob[p0:p0 + 64, :, c0_:c0_ + 256].rearrange("p b (e q) -> p e b q", q=128)
                gg = G[p0:p0 + 64, 2 * kvh:2 * kvh + 2, :].rearrange("p e (b q) -> p e b q", q=128)
                tt("pool", rz, rz, zz, ALU.mult, reads=["Rn", "Zg"], writes=["Rn"])
                tt("dve", gg, oo_, rz, ALU.mult, reads=["Ob", "Rn"], writes=gw)

        load_q(0)
        load_z(0)
        stageA(0)
        stageA(1)
        tgen = None
        for n in range(len(its)):
            g, kvh, qbl = its[n]
            if n + 2 < len(its):
                stageA(n + 2)
            stageB(n)
            if kvh == 3 and qbl == 3:
                tail(li, g, Gs[g % 2], f"G{g % 2}", hsrc, hdst, tw, is_last)
        T.barrier()

    def layer_B(li, hsrc, hdst, is_last):
        j = li // 2
        SC = float(96.0 ** -0.5)
        A.reset(base_mark)
        pm = A.mark()
        W = A.alloc("Win", [128, 8, 1728], BF16)
        Wq = A.alloc("Wq", [128, 3, 2048], BF16)
        wst = [A.alloc("wst", [128, 1024], F32) for _ in range(6)]
        hTs = [A.alloc("hT", [128, 8, TG], F32) for _ in range(2)]
        sqb = [A.alloc("sq", [128, TG], BF16) for _ in range(2)]
        rs = [A.alloc("r", [128, TG], F32) for _ in range(2)]
        r = A.alloc("r2", [128, TG], F32)
        r3 = A.alloc("r3", [128, TG], F32)
        sqb2 = [A.alloc("sq2", [128, TG], BF16) for _ in range(2)]
        us = [A.alloc("u", [128, 8, TG], BF16) for _ in range(2)]
        cq32 = A.alloc("cq32", [128, 3, TG], F32)
        ckv32 = A.alloc("ckv32", [128, 2, TG], F32)
        cqn = A.alloc("cqn", [128, 3, TG], BF16)
        ckvn = A.alloc("ckvn", [128, 2, TG], BF16)
        Cgs = [A.alloc("Cg", [128, TG], F32) for _ in range(2)]
        Sgs = [A.alloc("Sg", [128, TG], F32) for _ in range(2)]
        t1 = [A.alloc("t1", [128, TG], F32) for _ in range(2)]
        t2 = [A.alloc("t2", [128, TG], F32) for _ in range(2)]
        qst = [A.alloc("qst", [128, TG], BF16) for _ in range(3)]
        zst = [A.alloc("zst", [128, TG], BF16) for _ in range(3)]
        krst = A.alloc("krst", [128, TG], BF16)
        load_w(W, b_w_in[j], 8, 1728, V_NORM + 8 * li, "W", wst)
        load_w(Wq, b_w_qb[j], 3, 2048, V_QN + 3 * j, "Wq", wst)
        def prepB(g):
            par = g % 2
            gs_ = slice(g * TG, (g + 1) * TG)
            dma(hTs[par], hview(hsrc, g), reads=[("hd", g)],
                writes=[(f"h{par}", c) for c in range(8)])
            dma(Cgs[par], ropeC_d[:, gs_], writes=[f"Cg{par}"])
            dma(Sgs[par], ropeS_d[:, gs_], writes=[f"Sg{par}"])

        def prepB_norm(g):
            par = g % 2
            norm_group(hTs[par], f"h{par}", us[par], f"u{par}", 8, rs[par], sqb, TG, D, 32.0,
                       rtok=f"r{par}")
        prepB(0)
        prepB_norm(0)
        for g in range(NG):
            gs = slice(g * TG, (g + 1) * TG)
            if g + 1 < NG:
                prepB(g + 1)
            u = us[g % 2]
            ut = f"u{g % 2}"
            Cg, Sg = Cgs[g % 2], Sgs[g % 2]
            cgt, sgt = f"Cg{g % 2}", f"Sg{g % 2}"

            def proj(co, m, bank):
                for c in range(8):
                    mm(ps[0:m, bank, :], W[:, c, co:co + m], u[:, c, :], c == 0, c == 7,
                       reads=["W", (ut, c)], writes=[("ps", bank)])
            for cc in range(3):
                bank = nb()
                proj(cc * 128, 128, bank)
                act(cq32[:, cc, :], ps[:, bank, :], AF.Copy, reads=[("ps", bank)],
                    writes=[("cq32", cc)])
            for cc in range(2):
                bank = nb()
                proj(384 + cc * 128, 128, bank)
                act(ckv32[:, cc, :], ps[:, bank, :], AF.Copy, reads=[("ps", bank)],
                    writes=[("ckv32", cc)])
            bA = nb()
            proj(640, 32, bA)
            bB = nb()
            proj(672, 32, bB)
            tt("dve", t1[0][0:32, :], ps[0:32, bA, :], Cg[0:32, :], ALU.mult,
               reads=[("ps", bA), cgt], writes=[("t1", 0)])
            tt("dve", t2[0][0:32, :], ps[0:32, bB, :], Sg[0:32, :], ALU.mult,
               reads=[("ps", bB), sgt], writes=[("t2", 0)])
            tt("pool", krst[0:32, :], t1[0][0:32, :], t2[0][0:32, :], ALU.add,
               reads=[("t1", 0), ("t2", 0)], writes=["krst"])
            dma(lat_src[2][:, gs], krst[0:32, :], reads=["krst"], writes=["latsrc"])
            def both_norms():
                yield from norm_gen(cq32, "cq32", cqn, "cqn", 3, r, sqb2, TG, 384,
                                    float(np.sqrt(384.0)), rtok="r2", sqtok="sq2")
                yield from norm_gen(ckv32, "ckv32", ckvn, "ckvn", 2, r3, sqb2, TG, 256, 16.0,
                                    rtok="r3", sqtok="sq2")
            ng = both_norms()
            for oc in range(8):
                bank = nb()
                proj(704 + oc * 128, 128, bank)
                k = uniq() % 3
                act(zst[k], ps[:, bank, :], AF.Silu, reads=[("ps", bank)], writes=[("zst", k)])
                dma(ZS[oc * 128:(oc + 1) * 128, gs], zst[k], reads=[("zst", k)], writes=[("ZS", g)])
                next(ng, None)
                next(ng, None)
            for _ in ng:
                pass
            if g + 1 < NG:
                prepB_norm(g + 1)
            for c in range(2):
                dma(lat_src[c][:, gs], ckvn[:, c, :], reads=[("ckvn", c)], writes=["latsrc"])

            def qproj(co, bank):
                for c in range(3):
                    mm(ps[:, bank, :], Wq[:, c, co:co + 128], cqn[:, c, :], c == 0, c == 2,
                       reads=["Wq", ("cqn", c)], writes=[("ps", bank)])
            for oc in range(8):
                bank = nb()
                qproj(oc * 128, bank)
                k = uniq() % 3
                if oc % 2 == 0:
                    T.add("act", lambda e, o=qst[k], i_=ps[:, bank, :]: e.mul(o, i_, SC),
                          reads=[("ps", bank)], writes=[("qst", k)])
                else:
                    ts("dve", qst[k], ps[:, bank, :], SC, None, ALU.mult, reads=[("ps", bank)],
                       writes=[("qst", k)])
                dma(QN[oc * 128:(oc + 1) * 128, gs], qst[k], reads=[("qst", k)], writes=[("QN", g)])
            for jc in range(4):
                bA = nb()
                qproj(1024 + jc * 128, bA)
                bB = nb()
                qproj(1536 + jc * 128, bB)
                kk = jc % 2
                stt("dve", t1[kk], ps[:, bA, :], SC, Cg, ALU.mult, ALU.mult,
                    reads=[("ps", bA), cgt], writes=[("t1", kk)])
                stt("dve", t2[kk], ps[:, bB, :], SC, Sg, ALU.mult, ALU.mult,
                    reads=[("ps", bB), sgt], writes=[("t2", kk)])
                k = uniq() % 3
                tt("pool", qst[k], t1[kk], t2[kk], ALU.add,
                   reads=[("t1", kk), ("t2", kk)], writes=[("qst", k)])
                dma(QR[jc * 128:(jc + 1) * 128, gs], qst[k], reads=[("qst", k)], writes=[("QR", g)])
        T.barrier()
        for c in range(3):
            T.add("pool", lambda e, c=c: e.collective_compute("AllGather", ALU.bypass, replica_groups=RG,
                                                              ins=[lat_src[c]], outs=[lat_all[c]]),
                  reads=["latsrc"], writes=["latall"], kind="cc")
        A.reset(pm)
        Wkv = A.alloc("Wkv", [128, 2, 2048], BF16)
        wst = [A.alloc("wst", [128, 1024], F32) for _ in range(6)]
        latg = [A.alloc("latg", [128, 2, TG], BF16) for _ in range(2)]
        knst = [A.alloc("knst", [128, 8, TG], BF16) for _ in range(2)]
        Vst = [A.alloc("Vst", [128, 16, 4, 64], BF16) for _ in range(2)]
        load_w(Wkv, b_w_kvb[j], 2, 2048, V_KVN + 2 * j, "Wkv", wst)
        KNv = KN.rearrange("(c p) t -> p c t", p=128)
        for gg in range(16):
            rk, gl = gg // 8, gg % 8
            k = gg % 2
            for c in range(2):
                dma(latg[k][:, c, :], lat_all[c][rk * 128:(rk + 1) * 128, gl * TG:(gl + 1) * TG],
                    reads=["latall"], writes=[("latg", k)])
            for oc in range(8):
                bank = nb()
                for c in range(2):
                    mm(ps[:, bank, :], Wkv[:, c, oc * 128:(oc + 1) * 128], latg[k][:, c, :],
                       c == 0, c == 1, reads=["Wkv", ("latg", k)], writes=[("ps", bank)])
                if oc % 2 == 0:
                    act(knst[k][:, oc, :], ps[:, bank, :], AF.Copy, reads=[("ps", bank)],
                        writes=[("knst", k)])
                else:
                    cp("dve", knst[k][:, oc, :], ps[:, bank, :], reads=[("ps", bank)],
                       writes=[("knst", k)])
            dma(KNv[:, :, gg * TG:(gg + 1) * TG], knst[k], reads=[("knst", k)], writes=["KN"])
            for blk in range(4):
                for half in range(2):
                    bank = nb()
                    for c in range(2):
                        mm(ps[:, bank, :], latg[k][:, c, blk * 128:(blk + 1) * 128],
                           Wkv[:, c, 1024 + half * 512:1536 + half * 512], c == 0, c == 1,
                           reads=["Wkv", ("latg", k)], writes=[("ps", bank)])
                    pv = ps[:, bank, :].rearrange("p (h x) -> p h x", x=64)
                    h0 = half * 8
                    if half == 0:
                        act(Vst[k][:, h0:h0 + 8, blk, :], pv, AF.Copy,
                            reads=[("ps", bank)], writes=[("Vst", k)])
                    else:
                        cp("dve", Vst[k][:, h0:h0 + 8, blk, :], pv,
                           reads=[("ps", bank)], writes=[("Vst", k)])
            dma(VS[:, :, gg * 4:(gg + 1) * 4, :].rearrange("h p k x -> p h k x"), Vst[k],
                reads=[("Vst", k)], writes=["VS"])
        T.barrier()
        A.reset(pm)
        tw = alloc_tail(li, b_w_out[j], is_last)
        m3_ = A.mark()
        wst = [A.alloc("wst", [128, 1024], F32) for _ in range(6)]
        load_tail_w(li, tw, b_w_out[j], wst)
        T.barrier()
        A.reset(m3_)
        Kb = [A.alloc("Kb", [96, SEQ], BF16) for _ in range(2)]
        Vb = [A.alloc("Vb", [128, 64, 128], BF16) for _ in range(2)]
        Qb = [A.alloc("Qb", [96, 1024], BF16) for _ in range(2)]
        Pb = [A.alloc("Pb", [128, 1024], BF16) for _ in range(3)]
        G = A.alloc("G", [128, 8, 1024], BF16)
        Zc = [A.alloc("Zc", [128, 1024], BF16) for _ in range(2)]
        Rn = [A.alloc("Rn", [128, 1024], F32) for _ in range(2)]
        On = [A.alloc("On", [128, 1024], F32) for _ in range(2)]
        mset("pool", Vb[0][:, :, 64:128], 1.0, writes=[("Vb", 0)])
        mset("pool", Vb[1][:, :, 0:64], 1.0, writes=[("Vb", 1)])
        for k in range(2):
            for rk in range(2):
                dma(Kb[k][64:96, rk * TL:(rk + 1) * TL], lat_all[2][rk * 32:(rk + 1) * 32, :],
                    reads=["latall"], writes=[("Kb", k)])

        def load_head(qg, h):
            k = h % 2
            qs = slice(qg * 1024, (qg + 1) * 1024)
            dma(Kb[k][0:64, :], KN[h * 64:(h + 1) * 64, :], reads=["KN"], writes=[("Kb", k)])
            vc = 0 if k == 0 else 64
            dma(Vb[k][:, :, vc:vc + 64], VS[h], reads=["VS"], writes=[("Vb", k)])
            dma(Qb[k][0:64, :], QN[h * 64:(h + 1) * 64, qs], reads=[("QN", gq) for gq in (2 * qg, 2 * qg + 1)],
                writes=[("Qb", k)])
            dma(Qb[k][64:96, :], QR[h * 32:(h + 1) * 32, qs], reads=[("QR", gq) for gq in (2 * qg, 2 * qg + 1)],
                writes=[("Qb", k)])

        Osb = [A.alloc("Osb", [128, 1024], F32) for _ in range(2)]
        OB = 6
        for qg in range(4):
            items = [(h, kt) for h in range(16) for kt in range(64)]

            def smm(n):
                h, kt = items[n]
                k = h % 2
                sb = n % 3
                for hf in range(2):
                    mm(ps[:, 2 * sb + hf, :], Kb[k][0:96, kt * 128:(kt + 1) * 128],
                       Qb[k][0:96, hf * 512:(hf + 1) * 512], True, True,
                       reads=[("Kb", k), ("Qb", k)], writes=[("ps", 2 * sb + hf)])

            load_head(qg, 0)
            load_head(qg, 1)
            dma(Zc[0], ZS[0:128, qg * 1024:(qg + 1) * 1024],
                reads=[("ZS", 2 * qg), ("ZS", 2 * qg + 1)], writes=[("Zc", 0)])
            for n in range(3):
                smm(n)
            for n in range(len(items)):
                h, kt = items[n]
                k = h % 2
                sb = n % 3
                pb = n % 3
                if kt == 0:
                    if 1 <= h and h + 1 < 16:
                        load_head(qg, h + 1)
                    if h % 2 == 0 and h + 2 < 16:
                        pz = h // 2 + 1
                        dma(Zc[pz % 2], ZS[pz * 128:(pz + 1) * 128, qg * 1024:(qg + 1) * 1024],
                            reads=[("ZS", 2 * qg), ("ZS", 2 * qg + 1)], writes=[("Zc", pz % 2)])
                act(Pb[pb].rearrange("p (a b) -> p a b", b=512), ps[:, 2 * sb:2 * sb + 2, :], AF.Exp,
                    reads=[("ps", 2 * sb), ("ps", 2 * sb + 1)], writes=[("Pb", pb)])
                if n + 3 < len(items):
                    smm(n + 3)
                for hf in range(2):
                    mm(ps[:, OB + hf, :], Vb[k][:, kt, :], Pb[pb][:, hf * 512:(hf + 1) * 512],
                       kt == 0, kt == 63, reads=[("Vb", k), ("Pb", pb)], writes=[("ps", OB + hf)])
                if kt == 63:
                    if h % 2 == 0:
                        so, oo = 64, 0
                    else:
                        so, oo = 0, 64
                    ch = h // 2
                    zc = Zc[ch % 2]
                    cp("dve", Osb[k].rearrange("p (a b) -> p a b", b=512), ps[:, OB:OB + 2, :],
                       reads=[("ps", OB), ("ps", OB + 1)], writes=[("Osb", k)])
                    T.add("dve", lambda e, o=Rn[k][oo:oo + 64, :], i_=Osb[k][so:so + 64, :]:
                          e.reciprocal(out=o, in_=i_), reads=[("Osb", k)], writes=[("Rn", k)])
                    tt("dve", On[k][oo:oo + 64, :], Osb[k][oo:oo + 64, :], Rn[k][oo:oo + 64, :], ALU.mult,
                       reads=[("Osb", k), ("Rn", k)], writes=[("On", k)])
                    tt("pool", G[oo:oo + 64, ch, :], On[k][oo:oo + 64, :], zc[oo:oo + 64, :], ALU.mult,
                       reads=[("On", k), ("Zc", ch % 2)], writes=[("G", ch)])
            for s in range(2):
                g = 2 * qg + s
                Gs = G[:, :, s * TG:(s + 1) * TG]
                tail(li, g, Gs, "G", hsrc, hdst, tw, is_last)
        T.barrier()

    n = len(layers)
    for idx, li in enumerate(layers):
        is_last = final and idx == n - 1
        hsrc = hin if idx == 0 else hmid
        hdst = hout if idx == n - 1 else hmid
        if li % 2 == 0:
            layer_A(li, hsrc, hdst, is_last)
        else:
            layer_B(li, hsrc, hdst, is_last)
    T.finish()
    T.emit()
    return nc


def _host_inputs(inp):
    f = np.float32
    x = np.asarray(inp["x"], f)
    p = np.asarray(inp["p"], f)
    a_w_in = np.asarray(inp["a_w_in"], f)
    q, k, v, z = a_w_in[:, :, :1024], a_w_in[:, :, 1024:1280], a_w_in[:, :, 1280:1536], a_w_in[:, :, 1536:]
    kd = np.concatenate([np.concatenate([k[:, :, i * 64:(i + 1) * 64]] * 2, axis=2) for i in range(4)], axis=2)
    a_w_in2 = np.ascontiguousarray(np.concatenate([q, kd, v, z], axis=2))
    b_w_in = np.asarray(inp["b_w_in"], f)
    kr = b_w_in[:, :, 640:672]
    kr_sw = np.concatenate([kr[:, :, 16:], kr[:, :, :16]], axis=2)
    b_w_in2 = np.ascontiguousarray(np.concatenate(
        [b_w_in[:, :, :640], kr, kr_sw, b_w_in[:, :, 672:]], axis=2))
    wq = np.asarray(inp["b_w_qb"], f).reshape(2, 384, 16, 96)
    qn = wq[:, :, :, :64].reshape(2, 384, 1024)
    qr = wq[:, :, :, 64:]
    qr_sw = np.concatenate([qr[..., 16:], qr[..., :16]], axis=-1)
    b_w_qb2 = np.ascontiguousarray(np.concatenate(
        [qn, qr.reshape(2, 384, 512), qr_sw.reshape(2, 384, 512)], axis=2))
    wkv = np.asarray(inp["b_w_kvb"], f).reshape(2, 256, 16, 128)
    b_w_kvb2 = np.ascontiguousarray(np.concatenate(
        [wkv[..., :64].reshape(2, 256, 1024), wkv[..., 64:].reshape(2, 256, 1024)], axis=2))

    def cols(vv, nch):
        return np.asarray(vv, f).reshape(nch, 128).T
    vecs = np.zeros((128, V_N), f)
    for l in range(4):
        vecs[:, V_NORM + 8 * l:V_NORM + 8 * l + 8] = cols(inp["norm_g"][l], 8)
        vecs[:, V_PLE + 8 * l:V_PLE + 8 * l + 8] = cols(inp["ple_norm_g"][l], 8)
    vecs[:, V_FIN:V_FIN + 8] = cols(inp["final_norm_g"], 8)
    for jj in range(2):
        vecs[:, V_QN + 3 * jj:V_QN + 3 * jj + 3] = cols(inp["b_q_norm"][jj], 3)
        vecs[:, V_KVN + 2 * jj:V_KVN + 2 * jj + 2] = cols(inp["b_kv_norm"][jj], 2)
    sinkrep = np.ascontiguousarray(np.broadcast_to(
        np.asarray(inp["a_sink"], f).reshape(1, 32), (128, 32)))
    jj_, rr_, ii_ = np.meshgrid(np.arange(128), np.arange(3), np.arange(128), indexing="ij")
    dist = np.abs((rr_ - 1) * 128 + jj_ - ii_)
    distm = np.where(dist <= 128, dist, 30000).astype(f).reshape(128, 384)
    inv_freq = 10000.0 ** (-np.arange(16, dtype=np.float64) / 16.0)
    shared = dict(a_w_in2=a_w_in2, a_w_out=np.asarray(inp["a_w_out"], f), b_w_in2=b_w_in2,
                  b_w_qb2=b_w_qb2, b_w_kvb2=b_w_kvb2, b_w_out=np.asarray(inp["b_w_out"], f),
                  ple_w=np.asarray(inp["ple_w"], f), ple_w_gate=np.asarray(inp["ple_w_gate"], f),
                  vecs=vecs, sinkrep=sinkrep, distm=distm)
    per_core = []
    for c in range(NCORES):
        b, hf = c // 2, c % 2
        sl = slice(hf * TL, (hf + 1) * TL)
        pos = np.arange(hf * TL, (hf + 1) * TL, dtype=np.float64)
        ang = pos[None, :] * inv_freq[:, None]
        cc = np.cos(ang)
        ss = np.sin(ang)
        C32 = np.concatenate([cc, cc], axis=0)
        S32 = np.concatenate([-ss, ss], axis=0)
        d = dict(shared)
        d["ropeC"] = np.ascontiguousarray(np.tile(C32, (4, 1)).astype(f))
        d["ropeS"] = np.ascontiguousarray(np.tile(S32, (4, 1)).astype(f))
        selv = [0, 0, 0, 1] if hf == 0 else [1, 0, 0, 0]
        d["sel"] = np.ascontiguousarray(np.broadcast_to(np.asarray(selv, f)[None, :], (128, 4)))
        d["hin"] = np.ascontiguousarray(x[b, sl, :].T)
        d["pT"] = np.ascontiguousarray(p[:, b, sl, :].transpose(0, 2, 1))
        per_core.append(d)
    return per_core


_NC_CACHE = {}


def _get_nc(layers, final):
    key = (tuple(layers), final)
    if key not in _NC_CACHE:
        _NC_CACHE[key] = build_program(list(layers), final)
    return _NC_CACHE[key]


FUSED = True


def kernel(**inputs):
    per_core = _host_inputs(inputs)
    if FUSED:
        plan = [([0, 1, 2, 3], True)]
    else:
        plan = [([0], False), ([1], False), ([2], False), ([3], True)]
    for layers, final in plan:
        nc = _get_nc(layers, final)
        res = run_bass_kernel_spmd(nc, per_core, core_ids=list(range(NCORES)))
        for c in range(NCORES):
            per_core[c]["hin"] = np.asarray(res.results[c]["hout"], np.float32)
    out = np.empty((4, SEQ, D), np.float32)
    for c in range(NCORES):
        b, hf = c // 2, c % 2
        out[b, hf * TL:(hf + 1) * TL, :] = per_core[c]["hin"].T
    return out
```

```python
import numpy as np
import concourse.bass as bass
import concourse.mybir as mybir
from concourse.bass_utils import run_bass_kernel_spmd

F32 = mybir.dt.float32
BF16 = mybir.dt.bfloat16
AF = mybir.ActivationFunctionType
ALU = mybir.AluOpType

NCORES = 8
D = 1024
TL = 4096
SEQ = 8192
TG = 512
NG = TL // TG
EPS = 1e-6
SB_LO = 16640
SB_HI = 229376

V_NORM = 0
V_PLE = 32
V_FIN = 64
V_QN = 72
V_KVN = 78
V_N = 82


class Tracker:
    ENGS = ["sp", "act", "dve", "pool", "pe"]

    def __init__(self, nc, n_dma_sems=24):
        self.nc = nc
        self.I = []
        self.lw = {}
        self.rd = {}
        self.nds = n_dma_sems
        self.dma_count = 0
        self.dma_last = {}
        self.cc_count = 0
        self.cc_last = None
        self.last_stream = {}
        self.bar = {e: {} for e in self.ENGS}

    def _stream(self, ins):
        if ins["kind"] == "c":
            return ins["eng"]
        if ins["kind"] == "dma":
            return ("dma", ins["slot"])
        return ("cc",)

    MAXI = None

    def add(self, eng, fn, reads=(), writes=(), kind="c"):
        i = len(self.I)
        if Tracker.MAXI is not None and i >= Tracker.MAXI and fn is not None:
            return None
        ins = dict(eng=eng, fn=fn, kind=kind, sig=False)
        deps = {}

        def dep(j):
            s = self._stream(self.I[j])
            if deps.get(s, -1) < j:
                deps[s] = j

        if kind == "dma":
            slot = self.dma_count % self.nds
            ins["slot"] = slot
            ins["target"] = 16 * (self.dma_count // self.nds + 1)
            self.dma_count += 1
            if slot in self.dma_last:
                dep(self.dma_last[slot])
            self.dma_last[slot] = i
        elif kind == "cc":
            self.cc_count += 1
            ins["target"] = self.cc_count
            if self.cc_last is not None:
                dep(self.cc_last)
            self.cc_last = i
        me = self._stream(ins)

        def same_c(j):
            o = self.I[j]
            return kind == "c" and o["kind"] == "c" and o["eng"] == eng

        for t in reads:
            for s_, w in self.lw.get(t, {}).items():
                if same_c(w) and eng == "pe":
                    continue
                dep(w)
        for t in writes:
            for s_, w in self.lw.get(t, {}).items():
                if not (same_c(w) and eng == "pe"):
                    dep(w)
            for s_, r in self.rd.get(t, {}).items():
                dep(r)
        for s_, j in self.bar[eng].items():
            if deps.get(s_, -1) < j:
                deps[s_] = j
        self.bar[eng] = {}
        ins["deps"] = deps
        self.I.append(ins)
        for t in reads:
            self.rd.setdefault(t, {})[me] = i
        for t in writes:
            self.lw.setdefault(t, {})[me] = i
        self.last_stream[me] = i
        return i

    def barrier(self):
        for e in self.ENGS:
            for s, j in self.last_stream.items():
                if self.bar[e].get(s, -1) < j:
                    self.bar[e][s] = j

    def finish(self):
        self.barrier()
        self.add("sp", None)

    def emit(self):
        nc = self.nc
        for ins in self.I:
            for s, j in ins["deps"].items():
                self.I[j]["sig"] = True
        cnt = {e: 0 for e in self.ENGS}
        for ins in self.I:
            if ins["kind"] == "c" and ins["sig"]:
                cnt[ins["eng"]] += 1
                ins["tick"] = cnt[ins["eng"]]
        csem = {e: nc.alloc_semaphore(f"c_{e}") for e in ["act", "dve", "pool", "pe"]}
        dsem = [nc.alloc_semaphore(f"d_{k}") for k in range(self.nds)]
        ccsem = nc.alloc_semaphore("ccsem")
        I = self.I

        def run(engname, e):
            waited = {}
            for ins in I:
                if ins["eng"] != engname:
                    continue
                for s, j in ins["deps"].items():
                    p = I[j]
                    if p["kind"] == "c":
                        sem, val = csem[p["eng"]], p["tick"]
                    elif p["kind"] == "dma":
                        sem, val = dsem[p["slot"]], p["target"]
                    else:
                        sem, val = ccsem, p["target"]
                    if waited.get(s, 0) >= val:
                        continue
                    waited[s] = val
                    e.wait_ge(sem, val)
                if ins["fn"] is None:
                    continue
                bi = ins["fn"](e)
                if ins["kind"] == "dma":
                    bi.then_inc(dsem[ins["slot"]], 16)
                elif ins["kind"] == "cc":
                    bi.then_inc(ccsem, 1)
                elif ins["sig"]:
                    bi.then_inc(csem[engname], 1)

        with nc.Block() as block:
            @block.sync
            def _(e):
                run("sp", e)

            @block.scalar
            def _(e):
                run("act", e)

            @block.vector
            def _(e):
                run("dve", e)

            @block.gpsimd
            def _(e):
                run("pool", e)

            @block.tensor
            def _(e):
                run("pe", e)


class Arena:
    def __init__(self, nc):
        self.nc = nc
        self.cur = SB_LO
        self.uid = 0

    def alloc(self, name, shape, dtype):
        n = 1
        for s in shape[1:]:
            n *= s
        nb = n * (4 if dtype == F32 else 2)
        nb = (nb + 31) // 32 * 32
        off = self.cur
        assert off + nb <= SB_HI, f"SBUF overflow at {name}: {off + nb}"
        self.cur = off + nb
        self.uid += 1
        h = self.nc.alloc_sbuf_tensor_at(f"{name}_{self.uid}", list(shape), dtype, offset=off)
        return h.ap()

    def mark(self):
        return self.cur

    def reset(self, m):
        self.cur = m


def alibi_slope(h):
    return float(2.0 ** (-8.0 * (h + 1) / 16.0))


def build_program(layers, final, ncores=NCORES):
    nc = bass.Bass("TRN2", target_bir_lowering=False)
    T = Tracker(nc)
    A = Arena(nc)

    def din(name, shape, dt=F32):
        return nc.dram_tensor(name, list(shape), dt, kind="ExternalInput").ap()

    hin = din("hin", [D, TL])
    pT = din("pT", [4, 256, TL])
    vecs_d = din("vecs", [128, V_N])
    sink_d = din("sinkrep", [128, 32])
    distm_d = din("distm", [128, 384])
    sel_d = din("sel", [128, 4])
    ropeC_d = din("ropeC", [128, TL])
    ropeS_d = din("ropeS", [128, TL])
    a_w_in = din("a_w_in2", [2, D, 2816])
    a_w_out = din("a_w_out", [2, D, D])
    b_w_in = din("b_w_in2", [2, D, 1728])
    b_w_qb = din("b_w_qb2", [2, 384, 2048])
    b_w_kvb = din("b_w_kvb2", [2, 256, 2048])
    b_w_out = din("b_w_out", [2, D, D])
    ple_w = din("ple_w", [4, 256, D])
    ple_wg = din("ple_w_gate", [4, D, D])
    hout = nc.dram_tensor("hout", [D, TL], F32, kind="ExternalOutput").ap()

    hmid = nc.dram_tensor("hmid", [D, TL], F32).ap()
    QA = nc.dram_tensor("QA", [D, TL], BF16).ap()
    ZS = nc.dram_tensor("ZS", [D, TL], BF16).ap()
    AGW = 2176
    agA_src = nc.dram_tensor("agA_src", [128, AGW], BF16).ap()
    agA_dst = nc.dram_tensor("agA_dst", [256, AGW], BF16).ap()
    QN = nc.dram_tensor("QN", [D, TL], BF16).ap()
    QR = nc.dram_tensor("QR", [512, TL], BF16).ap()
    lat_src = [nc.dram_tensor(f"lat_src{c}", [128 if c < 2 else 32, TL], BF16).ap() for c in range(3)]
    lat_all = [nc.dram_tensor(f"lat_all{c}", [256 if c < 2 else 64, TL], BF16).ap() for c in range(3)]
    KN = nc.dram_tensor("KN", [D, SEQ], BF16).ap()
    VS = nc.dram_tensor("VS", [16, 128, 64, 64], BF16).ap()

    ps = nc.alloc_psum_tensor("ps", [128, 8, 512], F32).ap()
    RG = [[2 * i, 2 * i + 1] for i in range(ncores // 2)]

    state = dict(bank=0, k=0)

    def nb():
        b = state["bank"]
        state["bank"] = (b + 1) % 8
        return b

    def uniq():
        state["k"] += 1
        return state["k"]

    def dma(out, in_, reads=(), writes=()):
        T.add("sp", lambda e, out=out, in_=in_: e.dma_start(out=out, in_=in_),
              reads=reads, writes=writes, kind="dma")

    def mm(out, lhsT, rhs, start, stop, reads=(), writes=()):
        T.add("pe", lambda e, out=out, lhsT=lhsT, rhs=rhs, start=start, stop=stop:
              e.matmul(out, lhsT, rhs, start=start, stop=stop), reads=reads, writes=writes)

    def act(out, in_, func, reads=(), writes=(), scale=1.0):
        T.add("act", lambda e, out=out, in_=in_, func=func, scale=scale:
              e.activation(out=out, in_=in_, func=func, scale=scale), reads=reads, writes=writes)

    def tt(eng, out, in0, in1, op, reads=(), writes=()):
        T.add(eng, lambda e, out=out, in0=in0, in1=in1, op=op:
              e.tensor_tensor(out=out, in0=in0, in1=in1, op=op), reads=reads, writes=writes)

    def ts(eng, out, in0, s1, s2, op0, op1=None, reads=(), writes=()):
        if op1 is None:
            T.add(eng, lambda e, out=out, in0=in0, s1=s1, op0=op0:
                  e.tensor_scalar(out=out, in0=in0, scalar1=s1, scalar2=None, op0=op0),
                  reads=reads, writes=writes)
        else:
            T.add(eng, lambda e, out=out, in0=in0, s1=s1, s2=s2, op0=op0, op1=op1:
                  e.tensor_scalar(out=out, in0=in0, scalar1=s1, scalar2=s2, op0=op0, op1=op1),
                  reads=reads, writes=writes)

    def stt(eng, out, in0, scalar, in1, op0, op1, reads=(), writes=()):
        T.add(eng, lambda e, out=out, in0=in0, scalar=scalar, in1=in1, op0=op0, op1=op1:
              e.scalar_tensor_tensor(out=out, in0=in0, scalar=scalar, in1=in1, op0=op0, op1=op1),
              reads=reads, writes=writes)

    def cp(eng, out, in_, reads=(), writes=()):
        T.add(eng, lambda e, out=out, in_=in_: e.tensor_copy(out=out, in_=in_),
              reads=reads, writes=writes)

    def mset(eng, ap, val, writes=()):
        T.add(eng, lambda e, ap=ap, val=val: e.memset(ap, val), writes=writes)

    ones = A.alloc("ones", [128, 128], BF16)
    vecs = A.alloc("vecs", [128, V_N], F32)
    gf32 = A.alloc("gf32", [128, 8], F32)
    sinkrep = A.alloc("sinkrep", [128, 32], F32)
    esink = A.alloc("esink", [128, 32], F32)
    distm = A.alloc("distm", [128, 384], F32)
    sel = A.alloc("sel", [128, 4], F32)
    mset("pool", ones, 1.0, writes=["ones"])
    dma(vecs, vecs_d, writes=["vecs"])
    dma(sinkrep, sink_d, writes=["sinkrep"])
    dma(distm, distm_d, writes=["distm"])
    dma(sel, sel_d, writes=["sel"])
    ts("pool", gf32, vecs[:, V_FIN:V_FIN + 8], 1.0, None, ALU.mult, reads=["vecs"], writes=["gf32"])
    act(esink, sinkrep, AF.Exp, reads=["sinkrep"], writes=["esink"])
    base_mark = A.mark()

    def load_w(dst, src, KC, N, gcol, tok, wst):
        PW = 1024
        pieces = [(n0, min(n0 + PW, N)) for n0 in range(0, N, PW)]
        for c in range(KC):
            for (n0, n1) in pieces:
                k = uniq() % len(wst)
                w = n1 - n0
                dma(wst[k][:, 0:w], src[c * 128:(c + 1) * 128, n0:n1], writes=[("wst", k)])
                s1 = vecs[:, gcol + c:gcol + c + 1] if gcol is not None else 1.0
                if k % 2 == 0:
                    ts("dve", dst[:, c, n0:n1], wst[k][:, 0:w], s1, None, ALU.mult,
                       reads=[("wst", k), "vecs"], writes=[tok])
                else:
                    act(dst[:, c, n0:n1], wst[k][:, 0:w], AF.Copy, scale=s1,
                        reads=[("wst", k), "vecs"], writes=[tok])

    def norm_gen(src, stok, dst, dtok, nch, r, sqb, wts, dim, mult, rtok="r", nbk=None, sqtok="sq"):
        bank = (nbk or nb)()
        pend = []
        for c in range(nch):
            k = c % 2
            eng = "pool" if c % 2 == 0 else "dve"
            tt(eng, sqb[k], src[:, c, :], src[:, c, :], ALU.mult,
               reads=[(stok, c)], writes=[(sqtok, k)])
            pend.append((c, k))
            if len(pend) == 2 or c == nch - 1:
                yield
                for (c_, k_) in pend:
                    mm(ps[:, bank, 0:wts], ones, sqb[k_][:, 0:wts], c_ == 0, c_ == nch - 1,
                       reads=[(sqtok, k_), "ones"], writes=[("ps", bank)])
                pend = []
        yield
        m2 = float(mult) ** 2
        ts("dve", r, ps[:, bank, 0:wts], 1.0 / m2, float(dim * EPS) / m2, ALU.mult, ALU.add,
           reads=[("ps", bank)], writes=[rtok])
        yield
        act(r, r, AF.Ln, reads=[rtok], writes=[rtok])
        act(r, r, AF.Exp, scale=-0.5, reads=[rtok], writes=[rtok])
        yield
        for c in range(nch):
            eng = "dve" if c % 2 == 0 else "pool"
            tt(eng, dst[:, c, :], src[:, c, :], r, ALU.mult,
               reads=[(stok, c), rtok], writes=[(dtok, c)])
            if c % 4 == 3:
                yield

    def norm_group(*a, **kw):
        for _ in norm_gen(*a, **kw):
            pass

    def hview(ap, g):
        return ap.rearrange("(c p) t -> p c t", p=128)[:, :, g * TG:(g + 1) * TG]

    def tail_gen(li, g, G, gtok, hsrc, hdst, tw, is_last, nbk=None):
        nbk = nbk or nb
        hT, sqb, r, n2, pst, pbf, gate, tmpb, obuf = (tw[k] for k in
            ["hT", "sqb", "r", "n2", "pst", "pbf", "gate", "tmp", "obuf"])
        Wo, Wg, Wp = tw["Wo"], tw["Wg"], tw["Wp"]
        dma(hT, hview(hsrc, g), reads=[("hd", g)], writes=[("h", c) for c in range(8)])
        dma(pst, pT[li].rearrange("(c p) t -> p c t", p=128)[:, :, g * TG:(g + 1) * TG],
            writes=["pst"])
        yield
        cp("pool", pbf, pst, reads=["pst"], writes=["pbf"])
        banks = {}
        for oc in range(9):
            if oc < 8:
                bank = nbk()
                banks[oc] = bank
                for c in range(8):
                    mm(ps[:, bank, :], Wo[:, c, oc * 128:(oc + 1) * 128], G[:, c, :], c == 0, c == 7,
                       reads=["Wo", (gtok, c)], writes=[("ps", bank)])
            if oc >= 1:
                o = oc - 1
                tt("dve", hT[:, o, :], ps[:, banks[o], :], hT[:, o, :], ALU.add,
                   reads=[("ps", banks[o]), ("h", o)], writes=[("h", o)])
            yield
        for _ in norm_gen(hT, "h", n2, "n2", 8, r, sqb, TG, D, 32.0, nbk=nbk):
            yield
        b1s, b2s = {}, {}
        for oc in range(9):
            if oc < 8:
                b1 = nbk()
                b1s[oc] = b1
                for c in range(8):
                    mm(ps[:, b1, :], Wg[:, c, oc * 128:(oc + 1) * 128], n2[:, c, :], c == 0, c == 7,
                       reads=["Wg", ("n2", c)], writes=[("ps", b1)])
            if 1 <= oc < 9:
                o = oc - 1
                act(gate[0], ps[:, b1s[o], :], AF.Sigmoid, reads=[("ps", b1s[o])], writes=[("gate", 0)])
                b2 = nbk()
                b2s[o] = b2
                for c in range(2):
                    mm(ps[:, b2, :], Wp[:, c, o * 128:(o + 1) * 128], pbf[:, c, :], c == 0, c == 1,
                       reads=["Wp", "pbf"], writes=[("ps", b2)])
                tt("dve", tmpb[0], ps[:, b2, :], gate[0], ALU.mult,
                   reads=[("ps", b2), ("gate", 0)], writes=[("tmp", 0)])
                tt("pool", hT[:, o, :], hT[:, o, :], tmpb[0], ALU.add,
                   reads=[("h", o), ("tmp", 0)], writes=[("h", o)])
            yield
        if not is_last:
            dma(hview(hdst, g), hT, reads=[("h", c) for c in range(8)], writes=[("hd", g)])
        else:
            bank = nbk()
            for c in range(8):
                k = c % 2
                tt("pool" if c % 2 == 0 else "dve", sqb[k], hT[:, c, :], hT[:, c, :], ALU.mult,
                   reads=[("h", c)], writes=[("sq", k)])
                mm(ps[:, bank, :], ones, sqb[k], c == 0, c == 7,
                   reads=[("sq", k), "ones"], writes=[("ps", bank)])
            ts("dve", r, ps[:, bank, :], 1.0 / 1024.0, float(EPS), ALU.mult, ALU.add,
               reads=[("ps", bank)], writes=["r"])
            act(r, r, AF.Ln, reads=["r"], writes=["r"])
            act(r, r, AF.Exp, scale=-0.5, reads=["r"], writes=["r"])
            for c in range(8):
                k = c % 2
                stt("dve", obuf[k], hT[:, c, :], gf32[:, c:c + 1], r, ALU.mult, ALU.mult,
                    reads=[("h", c), "r", "gf32"], writes=[("ob", k)])
                dma(hout[c * 128:(c + 1) * 128, g * TG:(g + 1) * TG], obuf[k],
                    reads=[("ob", k)], writes=[("hd", g)])

    def tail(*a, **kw):
        for _ in tail_gen(*a, **kw):
            pass

    def alloc_tail(li, wo_src, is_last):
        tw = {}
        tw["Wo"] = A.alloc("Wo", [128, 8, D], BF16)
        tw["Wg"] = A.alloc("Wg", [128, 8, D], BF16)
        tw["Wp"] = A.alloc("Wp", [128, 2, D], BF16)
        tw["hT"] = A.alloc("hT", [128, 8, TG], F32)
        tw["sqb"] = [A.alloc("sq", [128, TG], BF16) for _ in range(2)]
        tw["r"] = A.alloc("r", [128, TG], F32)
        tw["n2"] = A.alloc("n2", [128, 8, TG], BF16)
        tw["pst"] = A.alloc("pst", [128, 2, TG], F32)
        tw["pbf"] = A.alloc("pbf", [128, 2, TG], BF16)
        tw["gate"] = [A.alloc("gate", [128, TG], F32)] * 2
        tw["tmp"] = [A.alloc("tmp", [128, TG], F32)] * 2
        tw["obuf"] = [A.alloc("obuf", [128, TG], F32) for _ in range(2)] if is_last else None
        return tw

    def load_tail_w(li, tw, wo_src, wst):
        load_w(tw["Wo"], wo_src, 8, D, None, "Wo", wst)
        load_w(tw["Wg"], ple_wg[li], 8, D, V_PLE + 8 * li, "Wg", wst)
        load_w(tw["Wp"], ple_w[li], 2, D, None, "Wp", wst)

    def layer_A(li, hsrc, hdst, is_last):
        j = li // 2
        A.reset(base_mark)
        Kd = A.alloc("Kd", [128, 4, 34 * 128], BF16)
        Va = A.alloc("Va", [128, 34, 576], BF16)
        pm = A.mark()
        W = A.alloc("Win", [128, 8, 2816], BF16)
        wst = [A.alloc("wst", [128, 1024], F32) for _ in range(6)]
        hTs = [A.alloc("hT", [128, 8, TG], F32) for _ in range(2)]
        sqb = [A.alloc("sq", [128, TG], BF16) for _ in range(2)]
        rs = [A.alloc("r", [128, TG], F32) for _ in range(2)]
        us = [A.alloc("u", [128, 8, TG], BF16) for _ in range(2)]
        qst = [A.alloc("qst", [128, TG], BF16) for _ in range(3)]
        zst = [A.alloc("zst", [128, TG], BF16) for _ in range(3)]
        mset("pool", Va, 1.0, writes=["Va"])
        load_w(W, a_w_in[j], 8, 2816, V_NORM + 8 * li, "W", wst)

        def prep_load(g):
            par = g % 2
            dma(hTs[par], hview(hsrc, g), reads=[("hd", g)],
                writes=[(f"h{par}", c) for c in range(8)])

        def prep_norm(g):
            par = g % 2
            norm_group(hTs[par], f"h{par}", us[par], f"u{par}", 8, rs[par], sqb, TG, D, 32.0,
                       rtok=f"r{par}")
        prep_load(0)
        prep_norm(0)
        for g in range(NG):
            if g + 1 < NG:
                prep_load(g + 1)
            u = us[g % 2]
            ut = f"u{g % 2}"
            for oc in range(8):
                bank = nb()
                for c in range(8):
                    mm(ps[:, bank, :], W[:, c, oc * 128:(oc + 1) * 128], u[:, c, :], c == 0, c == 7,
                       reads=["W", (ut, c)], writes=[("ps", bank)])
                k = uniq() % 3
                T.add("act", lambda e, o=qst[k], i_=ps[:, bank, :]: e.mul(o, i_, 0.125),
                      reads=[("ps", bank)], writes=[("qst", k)])
                dma(QA[oc * 128:(oc + 1) * 128, g * TG:(g + 1) * TG], qst[k],
                    reads=[("qst", k)], writes=[("QA", g)])
            if g + 1 < NG:
                prep_norm(g + 1)
            for kvh in range(4):
                bank = nb()
                co = 1024 + kvh * 128
                for c in range(8):
                    mm(ps[:, bank, :], W[:, c, co:co + 128], u[:, c, :], c == 0, c == 7,
                       reads=["W", (ut, c)], writes=[("ps", bank)])
                cp("dve", Kd[:, kvh, (1 + 4 * g) * 128:(5 + 4 * g) * 128], ps[:, bank, :],
                   reads=[("ps", bank)], writes=["Kd"])
            for blk in range(4):
                bank = nb()
                for c in range(8):
                    mm(ps[:, bank, 0:256], u[:, c, blk * 128:(blk + 1) * 128], W[:, c, 1536:1792],
                       c == 0, c == 7, reads=["W", (ut, c)], writes=[("ps", bank)])
                dst = Va[:, 1 + 4 * g + blk, 64:576].rearrange("p (k x) -> p k x", x=128)[:, :, 0:64]
                cp("dve", dst, ps[:, bank, 0:256].rearrange("p (k x) -> p k x", x=64),
                   reads=[("ps", bank)], writes=["Va"])
            for oc in range(8):
                bank = nb()
                co = 1792 + oc * 128
                for c in range(8):
                    mm(ps[:, bank, :], W[:, c, co:co + 128], u[:, c, :], c == 0, c == 7,
                       reads=["W", (ut, c)], writes=[("ps", bank)])
                k = uniq() % 3
                act(zst[k], ps[:, bank, :], AF.Silu, reads=[("ps", bank)], writes=[("zst", k)])
                dma(ZS[oc * 128:(oc + 1) * 128, g * TG:(g + 1) * TG], zst[k],
                    reads=[("zst", k)], writes=[("ZS", g)])
        dma(agA_src[:, 0:512].rearrange("p (k x) -> p k x", x=128), Kd[:, :, 128:256],
            reads=["Kd"], writes=["agsrc"])
        dma(agA_src[:, 512:1024].rearrange("p (k x) -> p k x", x=128), Kd[:, :, 32 * 128:33 * 128],
            reads=["Kd"], writes=["agsrc"])
        dma(agA_src[:, 1024:1600], Va[:, 1, :], reads=["Va"], writes=["agsrc"])
        dma(agA_src[:, 1600:2176], Va[:, 32, :], reads=["Va"], writes=["agsrc"])
        T.barrier()
        T.add("pool", lambda e: e.collective_compute("AllGather", ALU.bypass, replica_groups=RG,
                                                     ins=[agA_src], outs=[agA_dst]),
              reads=["agsrc"], writes=["agdst"], kind="cc")
        A.reset(pm)
        tw = alloc_tail(li, a_w_out[j], is_last)
        sinkV = A.alloc("sinkV", [2, 16, 128], BF16)
        pat = A.alloc("pat", [2, 2, 128], BF16)
        m2_ = A.mark()
        wst = [A.alloc("wst", [128, 1024], F32) for _ in range(6)]
        load_tail_w(li, tw, a_w_out[j], wst)
        GA = A.alloc("GA", [128, 2, AGW], BF16)
        htmp = A.alloc("htmp", [128, 576], BF16)
        dma(GA, agA_dst.rearrange("(r p) w -> p r w", p=128), reads=["agdst"], writes=["GA"])

        def halo(dst, lo, hi, s0, view):
            w = hi - lo
            ts("dve", htmp[:, 0:w], GA[:, 0, lo:hi], sel[:, s0:s0 + 1], None, ALU.mult,
               reads=["GA", "sel"], writes=["htmp"])
            t = htmp[:, 0:w]
            src = GA[:, 1, lo:hi]
            if view:
                t = t.rearrange("p (k x) -> p k x", x=128)
                src = src.rearrange("p (k x) -> p k x", x=128)
            stt("dve", dst, src, sel[:, s0 + 1:s0 + 2], t, ALU.mult, ALU.add,
                reads=["GA", "sel", "htmp"], writes=["Kd", "Va"])

        halo(Kd[:, :, 0:128], 512, 1024, 0, True)
        halo(Kd[:, :, 33 * 128:34 * 128], 0, 512, 2, True)
        halo(Va[:, 0, :], 1600, 2176, 0, False)
        halo(Va[:, 33, :], 1024, 1600, 2, False)
        shi32 = A.alloc("shi32", [1, 16, 128], F32)
        slo = A.alloc("slo", [1, 16, 128], BF16)
        src = esink[0:1, 16 * j:16 * j + 16].unsqueeze(2).to_broadcast([1, 16, 128])
        cp("dve", sinkV[0:1, :, :], src, reads=["esink"], writes=["sinkV"])
        cp("dve", shi32, sinkV[0:1, :, :], reads=["sinkV"], writes=["shi32"])
        tt("dve", slo, src, shi32, ALU.subtract, reads=["esink", "shi32"], writes=["slo"])
        dma(sinkV[1:2, :, :], slo[0:1, :, :], reads=["slo", "sinkV"], writes=["sinkV"])
        mset("pool", pat, 0.0, writes=["pat"])
        mset("pool", pat[0:2, 0, 64:128], 1.0, writes=["pat"])
        mset("pool", pat[0:2, 1, 0:64], 1.0, writes=["pat"])
        T.barrier()
        A.reset(m2_)
        Qk = [A.alloc("Qk", [64, 4, TG], BF16) for _ in range(2)]
        Zg1 = A.alloc("Zg", [128, 8, TG], BF16)
        Gs = [A.alloc("G", [128, 8, TG], BF16) for _ in range(2)]
        P = [A.alloc("P", [128, 3, 4, 128], BF16) for _ in range(3)]
        Ob = A.alloc("Ob", [128, 4, 512], F32)
        Rn = A.alloc("Rn", [128, 4, 256], F32)
        dv = distm.rearrange("p (r i) -> p r i", i=128)
        QAh = QA.rearrange("(h d) t -> d h t", d=64)
        v3 = lambda ap: ap.rearrange("p (a b) -> p a b", b=128)
        tb = dict(k=0)

        def nbt():
            tb["k"] = (tb["k"] + 1) % 3
            return 5 + tb["k"]

        its = [(g, kvh, qbl) for g in range(NG) for kvh in range(4) for qbl in range(4)]
        pairs = [(g, kvh) for g in range(NG) for kvh in range(4)]

        def load_q(pi):
            g, kvh = pairs[pi]
            dma(Qk[pi % 2], QAh[:, 4 * kvh:4 * kvh + 4, g * TG:(g + 1) * TG], reads=[("QA", g)],
                writes=[("Qk", pi % 2)])

        def load_z(g):
            dma(Zg1, hview(ZS, g), reads=[("ZS", g)], writes=["Zg"])

        def stageA(n):
            g, kvh, qbl = its[n]
            pi = g * 4 + kvh
            if qbl == 0 and pi + 1 < len(pairs):
                load_q(pi + 1)
            qk = Qk[pi % 2]
            qb = 4 * g + qbl
            s0 = 3 * (n % 2)
            sbk = [("ps", s0 + rr) for rr in range(3)]
            for rr in range(3):
                kblk = qb + rr
                mm(ps[:, s0 + rr, :].rearrange("p (h q) -> p h q", q=128),
                   Kd[0:64, kvh, kblk * 128:(kblk + 1) * 128],
                   qk[:, :, qbl * 128:(qbl + 1) * 128], True, True,
                   reads=["Kd", ("Qk", pi % 2)], writes=[("ps", s0 + rr)])
            for hh in range(4):
                stt("dve", ps[:, s0:s0 + 3, hh * 128:(hh + 1) * 128], dv, -alibi_slope(4 * kvh + hh),
                    ps[:, s0:s0 + 3, hh * 128:(hh + 1) * 128], ALU.mult, ALU.add,
                    reads=["distm"] + sbk, writes=sbk)
            act(P[n % 3], ps[:, s0:s0 + 3, :].rearrange("p r (h q) -> p r h q", q=128), AF.Exp,
                reads=sbk, writes=[("P", n % 3)])

        def stageB(n):
            g, kvh, qbl = its[n]
            qb = 4 * g + qbl
            ob = 6 + n % 2
            Pn = P[n % 3]
            G = Gs[g % 2]
            if kvh == 0 and qbl == 0 and g > 0:
                load_z(g)
            for par in range(2):
                c0 = 64 + 128 * kvh if par == 0 else 128 * kvh
                oreg = ps[:, ob, par * 256:(par + 1) * 256].rearrange("p (h q) -> p h q", q=128)
                for rr in range(3):
                    mm(oreg, Va[:, qb + rr, c0:c0 + 128], Pn[:, rr, par:4:2, :], rr == 0, False,
                       reads=["Va", ("P", n % 3)], writes=[("ps", ob)])
                h0 = 4 * kvh + par
                mm(oreg, pat[0:2, par, :], sinkV[0:2, h0:h0 + 3:2, :], False, True,
                   reads=["sinkV", "pat"], writes=[("ps", ob)])
            cp("dve", Ob[:, qbl, :], ps[:, ob, :], reads=[("ps", ob)], writes=["Ob"])
            if qbl != 3:
                return
            act(Ob[64:128, :, 0:256], Ob[64:128, :, 0:256], AF.Ln, reads=["Ob"], writes=["Ob"])
            act(Ob[0:64, :, 256:512], Ob[0:64, :, 256:512], AF.Ln, reads=["Ob"], writes=["Ob"])
            act(Rn[0:64, :, :], Ob[64:128, :, 0:256], AF.Exp, scale=-1.0, reads=["Ob"],
                writes=["Rn"])
            act(Rn[64:128, :, :], Ob[0:64, :, 256:512], AF.Exp, scale=-1.0, reads=["Ob"],
                writes=["Rn"])
            gt = f"G{g % 2}"
            gw = [(gt, 2 * kvh), (gt, 2 * kvh + 1)]
            for (p0, c0_) in ((0, 0), (64, 256)):
                rz = Rn[p0:p0 + 64, :, :].rearrange("p b (e q) -> p e b q", q=128)
                zz = Zg1[p0:p0 + 64, 2 * kvh:2 * kvh + 2, :].rearrange("p e (b q) -> p e b q", q=128)
                oo_ = Ob[p0:p0 + 64, :, c0_:c0_ + 256].rearrange("p b (e q) -> p e b q", q=128)
                gg = G[p0:p0 + 64, 2 * kvh:2 * kvh + 2, :].rearrange("p e (b q) -> p e b q", q=128)
                tt("pool", rz, rz, zz, ALU.mult, reads=["Rn", "Zg"], writes=["Rn"])
                tt("dve", gg, oo_, rz, ALU.mult, reads=["Ob", "Rn"], writes=gw)

        load_q(0)
        load_z(0)
        stageA(0)
        stageA(1)
        tgen = None
        for n in range(len(its)):
            g, kvh, qbl = its[n]
            if n + 2 < len(its):
                stageA(n + 2)
            stageB(n)
            if kvh == 3 and qbl == 3:
                tail(li, g, Gs[g % 2], f"G{g % 2}", hsrc, hdst, tw, is_last)
        T.barrier()

    def layer_B(li, hsrc, hdst, is_last):
        j = li // 2
        SC = float(96.0 ** -0.5)
        A.reset(base_mark)
        pm = A.mark()
        W = A.alloc("Win", [128, 8, 1728], BF16)
        Wq = A.alloc("Wq", [128, 3, 2048], BF16)
        wst = [A.alloc("wst", [128, 1024], F32) for _ in range(6)]
        hTs = [A.alloc("hT", [128, 8, TG], F32) for _ in range(2)]
        sqb = [A.alloc("sq", [128, TG], BF16) for _ in range(2)]
        rs = [A.alloc("r", [128, TG], F32) for _ in range(2)]
        r = A.alloc("r2", [128, TG], F32)
        r3 = A.alloc("r3", [128, TG], F32)
        sqb2 = [A.alloc("sq2", [128, TG], BF16) for _ in range(2)]
        us = [A.alloc("u", [128, 8, TG], BF16) for _ in range(2)]
        cq32 = A.alloc("cq32", [128, 3, TG], F32)
        ckv32 = A.alloc("ckv32", [128, 2, TG], F32)
        cqn = A.alloc("cqn", [128, 3, TG], BF16)
        ckvn = A.alloc("ckvn", [128, 2, TG], BF16)
        Cgs = [A.alloc("Cg", [128, TG], F32) for _ in range(2)]
        Sgs = [A.alloc("Sg", [128, TG], F32) for _ in range(2)]
        t1 = [A.alloc("t1", [128, TG], F32) for _ in range(2)]
        t2 = [A.alloc("t2", [128, TG], F32) for _ in range(2)]
        qst = [A.alloc("qst", [128, TG], BF16) for _ in range(3)]
        zst = [A.alloc("zst", [128, TG], BF16) for _ in range(3)]
        krst = A.alloc("krst", [128, TG], BF16)
        load_w(W, b_w_in[j], 8, 1728, V_NORM + 8 * li, "W", wst)
        load_w(Wq, b_w_qb[j], 3, 2048, V_QN + 3 * j, "Wq", wst)
        def prepB(g):
            par = g % 2
            gs_ = slice(g * TG, (g + 1) * TG)
            dma(hTs[par], hview(hsrc, g), reads=[("hd", g)],
                writes=[(f"h{par}", c) for c in range(8)])
            dma(Cgs[par], ropeC_d[:, gs_], writes=[f"Cg{par}"])
            dma(Sgs[par], ropeS_d[:, gs_], writes=[f"Sg{par}"])

        def prepB_norm(g):
            par = g % 2
            norm_group(hTs[par], f"h{par}", us[par], f"u{par}", 8, rs[par], sqb, TG, D, 32.0,
                       rtok=f"r{par}")
        prepB(0)
        prepB_norm(0)
        for g in range(NG):
            gs = slice(g * TG, (g + 1) * TG)
            if g + 1 < NG:
                prepB(g + 1)
            u = us[g % 2]
            ut = f"u{g % 2}"
            Cg, Sg = Cgs[g % 2], Sgs[g % 2]
            cgt, sgt = f"Cg{g % 2}", f"Sg{g % 2}"

            def proj(co, m, bank):
                for c in range(8):
                    mm(ps[0:m, bank, :], W[:, c, co:co + m], u[:, c, :], c == 0, c == 7,
                       reads=["W", (ut, c)], writes=[("ps", bank)])
            for cc in range(3):
                bank = nb()
                proj(cc * 128, 128, bank)
                act(cq32[:, cc, :], ps[:, bank, :], AF.Copy, reads=[("ps", bank)],
                    writes=[("cq32", cc)])
            for cc in range(2):
                bank = nb()
                proj(384 + cc * 128, 128, bank)
                act(ckv32[:, cc, :], ps[:, bank, :], AF.Copy, reads=[("ps", bank)],
                    writes=[("ckv32", cc)])
            bA = nb()
            proj(640, 32, bA)
            bB = nb()
            proj(672, 32, bB)
            tt("dve", t1[0][0:32, :], ps[0:32, bA, :], Cg[0:32, :], ALU.mult,
               reads=[("ps", bA), cgt], writes=[("t1", 0)])
            tt("dve", t2[0][0:32, :], ps[0:32, bB, :], Sg[0:32, :], ALU.mult,
               reads=[("ps", bB), sgt], writes=[("t2", 0)])
            tt("pool", krst[0:32, :], t1[0][0:32, :], t2[0][0:32, :], ALU.add,
               reads=[("t1", 0), ("t2", 0)], writes=["krst"])
            dma(lat_src[2][:, gs], krst[0:32, :], reads=["krst"], writes=["latsrc"])
            def both_norms():
                yield from norm_gen(cq32, "cq32", cqn, "cqn", 3, r, sqb2, TG, 384,
                                    float(np.sqrt(384.0)), rtok="r2", sqtok="sq2")
                yield from norm_gen(ckv32, "ckv32", ckvn, "ckvn", 2, r3, sqb2, TG, 256, 16.0,
                                    rtok="r3", sqtok="sq2")
            ng = both_norms()
            for oc in range(8):
                bank = nb()
                proj(704 + oc * 128, 128, bank)
                k = uniq() % 3
                act(zst[k], ps[:, bank, :], AF.Silu, reads=[("ps", bank)], writes=[("zst", k)])
                dma(ZS[oc * 128:(oc + 1) * 128, gs], zst[k], reads=[("zst", k)], writes=[("ZS", g)])
                next(ng, None)
                next(ng, None)
            for _ in ng:
                pass
            if g + 1 < NG:
                prepB_norm(g + 1)
            for c in range(2):
                dma(lat_src[c][:, gs], ckvn[:, c, :], reads=[("ckvn", c)], writes=["latsrc"])

            def qproj(co, bank):
                for c in range(3):
                    mm(ps[:, bank, :], Wq[:, c, co:co + 128], cqn[:, c, :], c == 0, c == 2,
                       reads=["Wq", ("cqn", c)], writes=[("ps", bank)])
            for oc in range(8):
                bank = nb()
                qproj(oc * 128, bank)
                k = uniq() % 3
                if oc % 2 == 0:
                    T.add("act", lambda e, o=qst[k], i_=ps[:, bank, :]: e.mul(o, i_, SC),
                          reads=[("ps", bank)], writes=[("qst", k)])
                else:
                    ts("dve", qst[k], ps[:, bank, :], SC, None, ALU.mult, reads=[("ps", bank)],
                       writes=[("qst", k)])
                dma(QN[oc * 128:(oc + 1) * 128, gs], qst[k], reads=[("qst", k)], writes=[("QN", g)])
            for jc in range(4):
                bA = nb()
                qproj(1024 + jc * 128, bA)
                bB = nb()
                qproj(1536 + jc * 128, bB)
                kk = jc % 2
                stt("dve", t1[kk], ps[:, bA, :], SC, Cg, ALU.mult, ALU.mult,
                    reads=[("ps", bA), cgt], writes=[("t1", kk)])
                stt("dve", t2[kk], ps[:, bB, :], SC, Sg, ALU.mult, ALU.mult,
                    reads=[("ps", bB), sgt], writes=[("t2", kk)])
                k = uniq() % 3
                tt("pool", qst[k], t1[kk], t2[kk], ALU.add,
                   reads=[("t1", kk), ("t2", kk)], writes=[("qst", k)])
                dma(QR[jc * 128:(jc + 1) * 128, gs], qst[k], reads=[("qst", k)], writes=[("QR", g)])
        T.barrier()
        for c in range(3):
            T.add("pool", lambda e, c=c: e.collective_compute("AllGather", ALU.bypass, replica_groups=RG,
                                                              ins=[lat_src[c]], outs=[lat_all[c]]),
                  reads=["latsrc"], writes=["latall"], kind="cc")
        A.reset(pm)
        tw = alloc_tail(li, b_w_out[j], is_last)
        m3_ = A.mark()
        Wkv = A.alloc("Wkv", [128, 2, 2048], BF16)
        wst = [A.alloc("wst", [128, 1024], F32) for _ in range(6)]
        latg = [A.alloc("latg", [128, 2, TG], BF16) for _ in range(2)]
        knst = [A.alloc("knst", [128, 8, TG], BF16) for _ in range(2)]
        Vst = [A.alloc("Vst", [128, 16, 4, 64], BF16) for _ in range(2)]
        load_w(Wkv, b_w_kvb[j], 2, 2048, V_KVN + 2 * j, "Wkv", wst)
        load_tail_w(li, tw, b_w_out[j], wst)
        KNv = KN.rearrange("(c p) t -> p c t", p=128)
        for gg in range(16):
            rk, gl = gg // 8, gg % 8
            k = gg % 2
            for c in range(2):
                dma(latg[k][:, c, :], lat_all[c][rk * 128:(rk + 1) * 128, gl * TG:(gl + 1) * TG],
                    reads=["latall"], writes=[("latg", k)])
            for oc in range(8):
                bank = nb()
                for c in range(2):
                    mm(ps[:, bank, :], Wkv[:, c, oc * 128:(oc + 1) * 128], latg[k][:, c, :],
                       c == 0, c == 1, reads=["Wkv", ("latg", k)], writes=[("ps", bank)])
                if oc % 2 == 0:
                    act(knst[k][:, oc, :], ps[:, bank, :], AF.Copy, reads=[("ps", bank)],
                        writes=[("knst", k)])
                else:
                    cp("dve", knst[k][:, oc, :], ps[:, bank, :], reads=[("ps", bank)],
                       writes=[("knst", k)])
            dma(KNv[:, :, gg * TG:(gg + 1) * TG], knst[k], reads=[("knst", k)], writes=["KN"])
            for blk in range(4):
                for half in range(2):
                    bank = nb()
                    for c in range(2):
                        mm(ps[:, bank, :], latg[k][:, c, blk * 128:(blk + 1) * 128],
                           Wkv[:, c, 1024 + half * 512:1536 + half * 512], c == 0, c == 1,
                           reads=["Wkv", ("latg", k)], writes=[("ps", bank)])
                    pv = ps[:, bank, :].rearrange("p (h x) -> p h x", x=64)
                    h0 = half * 8
                    if half == 0:
                        act(Vst[k][:, h0:h0 + 8, blk, :], pv, AF.Copy,
                            reads=[("ps", bank)], writes=[("Vst", k)])
                    else:
                        cp("dve", Vst[k][:, h0:h0 + 8, blk, :], pv,
                           reads=[("ps", bank)], writes=[("Vst", k)])
            dma(VS[:, :, gg * 4:(gg + 1) * 4, :].rearrange("h p k x -> p h k x"), Vst[k],
                reads=[("Vst", k)], writes=["VS"])
        T.barrier()
        A.reset(m3_)
        Kb = [A.alloc("Kb", [96, SEQ], BF16) for _ in range(2)]
        Vb = [A.alloc("Vb", [128, 64, 128], BF16) for _ in range(2)]
        Qb = [A.alloc("Qb", [96, 1024], BF16) for _ in range(2)]
        Pb = [A.alloc("Pb", [128, 1024], BF16) for _ in range(3)]
        G = A.alloc("G", [128, 8, 1024], BF16)
        Zc = [A.alloc("Zc", [128, 1024], BF16) for _ in range(2)]
        Rn = [A.alloc("Rn", [128, 1024], F32) for _ in range(2)]
        On = [A.alloc("On", [128, 1024], F32) for _ in range(2)]
        mset("pool", Vb[0][:, :, 64:128], 1.0, writes=[("Vb", 0)])
        mset("pool", Vb[1][:, :, 0:64], 1.0, writes=[("Vb", 1)])
        for k in range(2):
            for rk in range(2):
                dma(Kb[k][64:96, rk * TL:(rk + 1) * TL], lat_all[2][rk * 32:(rk + 1) * 32, :],
                    reads=["latall"], writes=[("Kb", k)])

        def load_head(qg, h):
            k = h % 2
            qs = slice(qg * 1024, (qg + 1) * 1024)
            dma(Kb[k][0:64, :], KN[h * 64:(h + 1) * 64, :], reads=["KN"], writes=[("Kb", k)])
            vc = 0 if k == 0 else 64
            dma(Vb[k][:, :, vc:vc + 64], VS[h], reads=["VS"], writes=[("Vb", k)])
            dma(Qb[k][0:64, :], QN[h * 64:(h + 1) * 64, qs], reads=[("QN", gq) for gq in (2 * qg, 2 * qg + 1)],
                writes=[("Qb", k)])
            dma(Qb[k][64:96, :], QR[h * 32:(h + 1) * 32, qs], reads=[("QR", gq) for gq in (2 * qg, 2 * qg + 1)],
                writes=[("Qb", k)])

        Osb = [A.alloc("Osb", [128, 1024], F32) for _ in range(2)]
        OB = 6
        for qg in range(4):
            items = [(h, kt) for h in range(16) for kt in range(64)]

            def smm(n):
                h, kt = items[n]
                k = h % 2
                sb = n % 3
                for hf in range(2):
                    mm(ps[:, 2 * sb + hf, :], Kb[k][0:96, kt * 128:(kt + 1) * 128],
                       Qb[k][0:96, hf * 512:(hf + 1) * 512], True, True,
                       reads=[("Kb", k), ("Qb", k)], writes=[("ps", 2 * sb + hf)])

            load_head(qg, 0)
            load_head(qg, 1)
            dma(Zc[0], ZS[0:128, qg * 1024:(qg + 1) * 1024],
                reads=[("ZS", 2 * qg), ("ZS", 2 * qg + 1)], writes=[("Zc", 0)])
            for n in range(3):
                smm(n)
            for n in range(len(items)):
                h, kt = items[n]
                k = h % 2
                sb = n % 3
                pb = n % 3
                if kt == 0:
                    if 1 <= h and h + 1 < 16:
                        load_head(qg, h + 1)
                    if h % 2 == 0 and h + 2 < 16:
                        pz = h // 2 + 1
                        dma(Zc[pz % 2], ZS[pz * 128:(pz + 1) * 128, qg * 1024:(qg + 1) * 1024],
                            reads=[("ZS", 2 * qg), ("ZS", 2 * qg + 1)], writes=[("Zc", pz % 2)])
                act(Pb[pb].rearrange("p (a b) -> p a b", b=512), ps[:, 2 * sb:2 * sb + 2, :], AF.Exp,
                    reads=[("ps", 2 * sb), ("ps", 2 * sb + 1)], writes=[("Pb", pb)])
                if n + 3 < len(items):
                    smm(n + 3)
                for hf in range(2):
                    mm(ps[:, OB + hf, :], Vb[k][:, kt, :], Pb[pb][:, hf * 512:(hf + 1) * 512],
                       kt == 0, kt == 63, reads=[("Vb", k), ("Pb", pb)], writes=[("ps", OB + hf)])
                if kt == 63:
                    if h % 2 == 0:
                        so, oo = 64, 0
                    else:
                        so, oo = 0, 64
                    ch = h // 2
                    zc = Zc[ch % 2]
                    cp("dve", Osb[k].rearrange("p (a b) -> p a b", b=512), ps[:, OB:OB + 2, :],
                       reads=[("ps", OB), ("ps", OB + 1)], writes=[("Osb", k)])
                    T.add("dve", lambda e, o=Rn[k][oo:oo + 64, :], i_=Osb[k][so:so + 64, :]:
                          e.reciprocal(out=o, in_=i_), reads=[("Osb", k)], writes=[("Rn", k)])
                    tt("dve", On[k][oo:oo + 64, :], Osb[k][oo:oo + 64, :], Rn[k][oo:oo + 64, :], ALU.mult,
                       reads=[("Osb", k), ("Rn", k)], writes=[("On", k)])
                    tt("pool", G[oo:oo + 64, ch, :], On[k][oo:oo + 64, :], zc[oo:oo + 64, :], ALU.mult,
                       reads=[("On", k), ("Zc", ch % 2)], writes=[("G", ch)])
            for s in range(2):
                g = 2 * qg + s
                Gs = G[:, :, s * TG:(s + 1) * TG]
                tail(li, g, Gs, "G", hsrc, hdst, tw, is_last)
        T.barrier()

    n = len(layers)
    for idx, li in enumerate(layers):
        is_last = final and idx == n - 1
        hsrc = hin if idx == 0 else hmid
        hdst = hout if idx == n - 1 else hmid
        if li % 2 == 0:
            layer_A(li, hsrc, hdst, is_last)
        else:
            layer_B(li, hsrc, hdst, is_last)
    T.finish()
    T.emit()
    return nc


def _host_inputs(inp):
    f = np.float32
    x = np.asarray(inp["x"], f)
    p = np.asarray(inp["p"], f)
    a_w_in = np.asarray(inp["a_w_in"], f)
    q, k, v, z = a_w_in[:, :, :1024], a_w_in[:, :, 1024:1280], a_w_in[:, :, 1280:1536], a_w_in[:, :, 1536:]
    kd = np.concatenate([np.concatenate([k[:, :, i * 64:(i + 1) * 64]] * 2, axis=2) for i in range(4)], axis=2)
    a_w_in2 = np.ascontiguousarray(np.concatenate([q, kd, v, z], axis=2))
    b_w_in = np.asarray(inp["b_w_in"], f)
    kr = b_w_in[:, :, 640:672]
    kr_sw = np.concatenate([kr[:, :, 16:], kr[:, :, :16]], axis=2)
    b_w_in2 = np.ascontiguousarray(np.concatenate(
        [b_w_in[:, :, :640], kr, kr_sw, b_w_in[:, :, 672:]], axis=2))
    wq = np.asarray(inp["b_w_qb"], f).reshape(2, 384, 16, 96)
    qn = wq[:, :, :, :64].reshape(2, 384, 1024)
    qr = wq[:, :, :, 64:]
    qr_sw = np.concatenate([qr[..., 16:], qr[..., :16]], axis=-1)
    b_w_qb2 = np.ascontiguousarray(np.concatenate(
        [qn, qr.reshape(2, 384, 512), qr_sw.reshape(2, 384, 512)], axis=2))
    wkv = np.asarray(inp["b_w_kvb"], f).reshape(2, 256, 16, 128)
    b_w_kvb2 = np.ascontiguousarray(np.concatenate(
        [wkv[..., :64].reshape(2, 256, 1024), wkv[..., 64:].reshape(2, 256, 1024)], axis=2))

    def cols(vv, nch):
        return np.asarray(vv, f).reshape(nch, 128).T
    vecs = np.zeros((128, V_N), f)
    for l in range(4):
        vecs[:, V_NORM + 8 * l:V_NORM + 8 * l + 8] = cols(inp["norm_g"][l], 8)
        vecs[:, V_PLE + 8 * l:V_PLE + 8 * l + 8] = cols(inp["ple_norm_g"][l], 8)
    vecs[:, V_FIN:V_FIN + 8] = cols(inp["final_norm_g"], 8)
    for jj in range(2):
        vecs[:, V_QN + 3 * jj:V_QN + 3 * jj + 3] = cols(inp["b_q_norm"][jj], 3)
        vecs[:, V_KVN + 2 * jj:V_KVN + 2 * jj + 2] = cols(inp["b_kv_norm"][jj], 2)
    sinkrep = np.ascontiguousarray(np.broadcast_to(
        np.asarray(inp["a_sink"], f).reshape(1, 32), (128, 32)))
    jj_, rr_, ii_ = np.meshgrid(np.arange(128), np.arange(3), np.arange(128), indexing="ij")
    dist = np.abs((rr_ - 1) * 128 + jj_ - ii_)
    distm = np.where(dist <= 128, dist, 30000).astype(f).reshape(128, 384)
    inv_freq = 10000.0 ** (-np.arange(16, dtype=np.float64) / 16.0)
    shared = dict(a_w_in2=a_w_in2, a_w_out=np.asarray(inp["a_w_out"], f), b_w_in2=b_w_in2,
                  b_w_qb2=b_w_qb2, b_w_kvb2=b_w_kvb2, b_w_out=np.asarray(inp["b_w_out"], f),
                  ple_w=np.asarray(inp["ple_w"], f), ple_w_gate=np.asarray(inp["ple_w_gate"], f),
                  vecs=vecs, sinkrep=sinkrep, distm=distm)
    per_core = []
    for c in range(NCORES):
        b, hf = c // 2, c % 2
        sl = slice(hf * TL, (hf + 1) * TL)
        pos = np.arange(hf * TL, (hf + 1) * TL, dtype=np.float64)
        ang = pos[None, :] * inv_freq[:, None]
        cc = np.cos(ang)
        ss = np.sin(ang)
        C32 = np.concatenate([cc, cc], axis=0)
        S32 = np.concatenate([-ss, ss], axis=0)
        d = dict(shared)
        d["ropeC"] = np.ascontiguousarray(np.tile(C32, (4, 1)).astype(f))
        d["ropeS"] = np.ascontiguousarray(np.tile(S32, (4, 1)).astype(f))
        selv = [0, 0, 0, 1] if hf == 0 else [1, 0, 0, 0]
        d["sel"] = np.ascontiguousarray(np.broadcast_to(np.asarray(selv, f)[None, :], (128, 4)))
        d["hin"] = np.ascontiguousarray(x[b, sl, :].T)
        d["pT"] = np.ascontiguousarray(p[:, b, sl, :].transpose(0, 2, 1))
        per_core.append(d)
    return per_core


_NC_CACHE = {}


def _get_nc(layers, final):
    key = (tuple(layers), final)
    if key not in _NC_CACHE:
        _NC_CACHE[key] = build_program(list(layers), final)
    return _NC_CACHE[key]


FUSED = True


def kernel(**inputs):
    per_core = _host_inputs(inputs)
    if FUSED:
        plan = [([0, 1, 2, 3], True)]
    else:
        plan = [([0], False), ([1], False), ([2], False), ([3], True)]
    for layers, final in plan:
        nc = _get_nc(layers, final)
        res = run_bass_kernel_spmd(nc, per_core, core_ids=list(range(NCORES)))
        for c in range(NCORES):
            per_core[c]["hin"] = np.asarray(res.results[c]["hout"], np.float32)
    out = np.empty((4, SEQ, D), np.float32)
    for c in range(NCORES):
        b, hf = c // 2, c % 2
        out[b, hf * TL:(hf + 1) * TL, :] = per_core[c]["hin"].T
    return out
```
